# Optimizing a Trainium2 kernel written in Bass

```python
import jax, jax.numpy as jnp
from jax import lax
import numpy as np

D_MODEL = 1024
BATCH = 4
SEQ = 8192
DEPTH = 2

HEAD_DIM = 64
ATTN_WIDTH = D_MODEL // 2
N_Q_HEADS = ATTN_WIDTH // HEAD_DIM
N_KV_HEADS = 2
Q_PER_KV = N_Q_HEADS // N_KV_HEADS
KV_WIDTH = N_KV_HEADS * HEAD_DIM
WINDOW = 128
BLOCK = 128
CONV_WIDTH = D_MODEL // 4
CONV_KERNEL = 31
LRU_WIDTH = D_MODEL // 4
LRU_HEADS = 4
LRU_HEAD_DIM = LRU_WIDTH // LRU_HEADS
LRU_CONV_KERNEL = 4
LRU_C = 8.0
MIX_WIDTH = ATTN_WIDTH + CONV_WIDTH + LRU_WIDTH
IN_SPLIT_SIZES = (ATTN_WIDTH, KV_WIDTH, KV_WIDTH, CONV_WIDTH, CONV_WIDTH, LRU_WIDTH, LRU_WIDTH)
IN_WIDTH = sum(IN_SPLIT_SIZES)
IN_SPLIT_IDX = [int(v) for v in np.cumsum(IN_SPLIT_SIZES)[:-1]]
D_FF = 4 * D_MODEL
RMS_EPS = 1e-6
LN_EPS = 1e-5
MASK_VALUE = -1e30

kernel_name = "hymba_style_swa_conformer_rglru_hybrid"


def rms_norm(x, g):
    xf = x.astype(jnp.float32)
    y = xf * lax.rsqrt(jnp.mean(xf * xf, axis=-1, keepdims=True) + RMS_EPS)
    return (y * g.astype(jnp.float32)).astype(x.dtype)


def layer_norm(x, g, b):
    xf = x.astype(jnp.float32)
    mu = jnp.mean(xf, axis=-1, keepdims=True)
    xc = xf - mu
    y = xc * lax.rsqrt(jnp.mean(xc * xc, axis=-1, keepdims=True) + LN_EPS)
    return (y * g.astype(jnp.float32) + b.astype(jnp.float32)).astype(x.dtype)


def causal_depthwise_conv(x, w, b):
    k_width, c = w.shape
    out = lax.conv_general_dilated(
        x, w[:, None, :].astype(x.dtype), window_strides=(1,), padding=[(k_width - 1, 0)],
        dimension_numbers=("NWC", "WIO", "NWC"), feature_group_count=c)
    return out + b.astype(x.dtype)


def sliding_window_attention(q, k, v, sinks):
    b, s, _ = q.shape
    nb = s // BLOCK
    q = q.reshape(b, nb, BLOCK, N_KV_HEADS, Q_PER_KV, HEAD_DIM)
    k = k.reshape(b, nb, BLOCK, N_KV_HEADS, HEAD_DIM)
    v = v.reshape(b, nb, BLOCK, N_KV_HEADS, HEAD_DIM)
    k_band = jnp.concatenate([jnp.concatenate([jnp.zeros_like(k[:, :1]), k[:, :-1]], axis=1), k], axis=2)
    v_band = jnp.concatenate([jnp.concatenate([jnp.zeros_like(v[:, :1]), v[:, :-1]], axis=1), v], axis=2)
    scores = jnp.einsum("bnqhgd,bnkhd->bnhgqk", q, k_band).astype(jnp.float32) * (HEAD_DIM ** -0.5)
    blk = jnp.arange(nb)[:, None]
    q_pos = blk * BLOCK + jnp.arange(BLOCK)[None, :]
    k_pos = (blk - 1) * BLOCK + jnp.arange(2 * BLOCK)[None, :]
    diff = q_pos[:, :, None] - k_pos[:, None, :]
    mask = (diff >= 0) & (diff < WINDOW) & (k_pos[:, None, :] >= 0)
    scores = jnp.where(mask[None, :, None, None], scores, MASK_VALUE)
    sink = sinks.astype(jnp.float32).reshape(N_KV_HEADS, Q_PER_KV)[None, None, :, :, None, None]
    m = jnp.maximum(jnp.max(scores, axis=-1, keepdims=True), sink)
    p = jnp.exp(scores - m)
    probs = p / (jnp.sum(p, axis=-1, keepdims=True) + jnp.exp(sink - m))
    out = jnp.einsum("bnhgqk,bnkhd->bnqhgd", probs.astype(v.dtype), v_band)
    return out.reshape(b, s, ATTN_WIDTH)


def conformer_conv(u_val, u_gate, dw_w, dw_b, ln_g, ln_b):
    u = u_val * jax.nn.sigmoid(u_gate)
    u = causal_depthwise_conv(u, dw_w, dw_b)
    u = layer_norm(u, ln_g, ln_b)
    return jax.nn.silu(u)


def _linear_recurrence_combine(c1, c2):
    a1, b1 = c1
    a2, b2 = c2
    return a1 * a2, a2 * b1 + b2


def rglru_branch(u_x, u_gate, conv_w, conv_b, wa, ba, wx, bx, lam):
    xc = causal_depthwise_conv(u_x, conv_w, conv_b)
    b, s, _ = xc.shape
    xh = xc.reshape(b, s, LRU_HEADS, LRU_HEAD_DIM)
    r = jax.nn.sigmoid(jnp.einsum("bshi,hij->bshj", xh, wa) + ba).reshape(b, s, LRU_WIDTH)
    i = jax.nn.sigmoid(jnp.einsum("bshi,hij->bshj", xh, wx) + bx).reshape(b, s, LRU_WIDTH)
    log_a = (-LRU_C * r.astype(jnp.float32)) * jax.nn.softplus(-lam.astype(jnp.float32))
    a = jnp.exp(log_a)
    gated_x = jnp.sqrt(-jnp.expm1(2.0 * log_a)) * (i * xc).astype(jnp.float32)
    _, h = lax.associative_scan(_linear_recurrence_combine, (a, gated_x), axis=1)
    return h.astype(u_x.dtype) * jax.nn.gelu(u_gate)


def setup_inputs(seed: int = 0) -> dict:
    key = jax.random.key(seed)
    ks = jax.random.split(key, 24)
    f32 = jnp.float32

    def nrm(k, shape, scale):
        return jax.random.normal(k, shape, f32) * scale

    def gain(k, shape):
        return 1.0 + 0.02 * jax.random.normal(k, shape, f32)

    a0 = jax.random.uniform(ks[14], (DEPTH, LRU_WIDTH), f32, 0.9, 0.999)
    s0 = a0 ** (1.0 / LRU_C)
    lru_lambda = jnp.log(s0) - jnp.log1p(-s0)
    return {
        "x": jax.random.normal(ks[0], (BATCH, SEQ, D_MODEL), f32),
        "norm1": gain(ks[1], (DEPTH, D_MODEL)),
        "w_in": nrm(ks[2], (DEPTH, D_MODEL, IN_WIDTH), D_MODEL ** -0.5),
        "attn_sinks": nrm(ks[3], (DEPTH, N_Q_HEADS), 0.5),
        "conv_dw_w": nrm(ks[4], (DEPTH, CONV_KERNEL, CONV_WIDTH), CONV_KERNEL ** -0.5),
        "conv_dw_b": nrm(ks[5], (DEPTH, CONV_WIDTH), 0.01),
        "conv_ln_g": gain(ks[6], (DEPTH, CONV_WIDTH)),
        "conv_ln_b": nrm(ks[7], (DEPTH, CONV_WIDTH), 0.01),
        "lru_conv_w": nrm(ks[8], (DEPTH, LRU_CONV_KERNEL, LRU_WIDTH), LRU_CONV_KERNEL ** -0.5),
        "lru_conv_b": nrm(ks[9], (DEPTH, LRU_WIDTH), 0.01),
        "lru_wa": nrm(ks[10], (DEPTH, LRU_HEADS, LRU_HEAD_DIM, LRU_HEAD_DIM), LRU_HEAD_DIM ** -0.5),
        "lru_ba": nrm(ks[11], (DEPTH, LRU_HEADS, LRU_HEAD_DIM), 0.01),
        "lru_wx": nrm(ks[12], (DEPTH, LRU_HEADS, LRU_HEAD_DIM, LRU_HEAD_DIM), LRU_HEAD_DIM ** -0.5),
        "lru_bx": nrm(ks[13], (DEPTH, LRU_HEADS, LRU_HEAD_DIM), 0.01),
        "lru_lambda": lru_lambda,
        "mix_norm": gain(ks[15], (DEPTH, MIX_WIDTH)),
        "w_out": nrm(ks[16], (DEPTH, MIX_WIDTH, D_MODEL), MIX_WIDTH ** -0.5),
        "norm2": gain(ks[17], (DEPTH, D_MODEL)),
        "w_up": nrm(ks[18], (DEPTH, D_MODEL, D_FF), D_MODEL ** -0.5),
        "w_down": nrm(ks[19], (DEPTH, D_FF, D_MODEL), D_FF ** -0.5),
        "final_norm": gain(ks[20], (D_MODEL,)),
    }


def reference(x, norm1, w_in, attn_sinks, conv_dw_w, conv_dw_b, conv_ln_g, conv_ln_b,
              lru_conv_w, lru_conv_b, lru_wa, lru_ba, lru_wx, lru_bx, lru_lambda,
              mix_norm, w_out, norm2, w_up, w_down, final_norm):
    h = x
    a_end = ATTN_WIDTH
    c_end = ATTN_WIDTH + CONV_WIDTH
    for l in range(DEPTH):
        hn = rms_norm(h, norm1[l])
        z = hn @ w_in[l]
        q, k, v, c_val, c_gate, r_x, r_gate = jnp.split(z, IN_SPLIT_IDX, axis=-1)
        y_attn = sliding_window_attention(q, k, v, attn_sinks[l])
        y_conv = conformer_conv(c_val, c_gate, conv_dw_w[l], conv_dw_b[l], conv_ln_g[l], conv_ln_b[l])
        y_lru = rglru_branch(r_x, r_gate, lru_conv_w[l], lru_conv_b[l], lru_wa[l], lru_ba[l],
                             lru_wx[l], lru_bx[l], lru_lambda[l])
        g = mix_norm[l]
        y = jnp.concatenate([rms_norm(y_attn, g[:a_end]),
                             rms_norm(y_conv, g[a_end:c_end]),
                             rms_norm(y_lru, g[c_end:])], axis=-1)
        h = h + y @ w_out[l]
        hn = rms_norm(h, norm2[l])
        h = h + jnp.square(jax.nn.relu(hn @ w_up[l])) @ w_down[l]
    return rms_norm(h, final_norm)
```

```python
import numpy as np
from contextlib import ExitStack
import concourse.bass as bass
import concourse.mybir as mybir
from concourse.bass_utils import run_bass_kernel_spmd

F32 = mybir.dt.float32
BF16 = mybir.dt.bfloat16
AF = mybir.ActivationFunctionType
ALU = mybir.AluOpType
AX = mybir.AxisListType
GELU = AF.Gelu_apprx_tanh

P = 128
D = 1024
KC = 8
TT = 512
NBLK = 4
UPL = 87
RF = 4
RG = 6
GRP = 6
PL = 108
NPV = 2 * PL + 9
O_FLAG = 2 * PL + 8
SAME_ENGINE_SYNC = True
LIST_SCHED = True
ENG_SCALE = {"pe": 0.92, "dve": 1.25, "act": 1.33, "pool": 1.21, "sp": 1.0}
XLAT = 1.7
FFN_PRIO_BIAS = 150

O_G1, O_G2, O_MG, O_CW, O_CB, O_LNG, O_LNB, O_LW, O_LB, O_BA, O_BX, O_LAM = (
    0, 8, 16, 24, 86, 88, 90, 92, 100, 102, 104, 106)


class Op:
    __slots__ = ("eng", "fn", "deps", "need_inc", "count", "dma_sem", "dur", "lat", "idx")

    def __init__(self, eng, fn, dma_sem, dur, lat):
        self.eng = eng
        self.fn = fn
        self.deps = set()
        self.need_inc = False
        self.count = 0
        self.dma_sem = dma_sem
        self.dur = dur
        self.lat = lat
        self.idx = 0


class Sched:
    def __init__(self):
        self.ops = []
        self.last_writer = {}
        self.readers = {}
        self.prio_bias = 0

    DUR = {"pe": 0.25, "act": 0.6, "dve": 0.68, "pool": 1.0, "sp": 0.15}

    def add(self, eng, fn, reads=(), writes=(), dma_sem=None, c=None, lat=None):
        if c is None:
            c = self.DUR[eng]
        c *= ENG_SCALE[eng]
        if lat is None:
            lat = 3.0 if dma_sem is not None else XLAT
        op = Op(eng, fn, dma_sem, c, lat)
        op.idx = len(self.ops) + self.prio_bias
        deps = op.deps
        for r in reads:
            lw = self.last_writer.get(r)
            if lw is not None:
                deps.add(lw)
        for w in writes:
            lw = self.last_writer.get(w)
            if lw is not None:
                deps.add(lw)
            for rd in self.readers.get(w, ()):
                deps.add(rd)
        for r in reads:
            self.readers.setdefault(r, []).append(op)
        for w in writes:
            self.last_writer[w] = op
            self.readers[w] = []
        deps.discard(op)
        self.ops.append(op)
        return op

    def list_schedule(self):
        ops = self.ops
        nd = {}
        users = {}
        for op in ops:
            nd[op] = len(op.deps)
            for d in op.deps:
                users.setdefault(d, []).append(op)
        rt = {}
        fin = {}
        cand = {}
        free = {}
        for op in ops:
            if nd[op] == 0:
                rt[op] = 0.0
                cand.setdefault(op.eng, []).append(op)
        order = {}
        n_left = len(ops)
        while n_left:
            best = None
            for e, lst in cand.items():
                if not lst:
                    continue
                te = free.get(e, 0.0)
                pick = None
                for op in lst:
                    r = rt[op]
                    key = (0.0, op.idx) if r <= te else (r - te, op.idx)
                    if pick is None or key < pick[0]:
                        pick = (key, op)
                op = pick[1]
                start = max(te, rt[op])
                if best is None or (start, op.idx) < best[0]:
                    best = ((start, op.idx), op, start)
            _, op, start = best
            e = op.eng
            cand[e].remove(op)
            f = start + op.dur
            free[e] = f
            fin[op] = f
            order.setdefault(e, []).append(op)
            n_left -= 1
            for u_ in users.get(op, ()):
                if u_.eng == e:
                    lat = 0.0 if e == "pe" else (op.lat if op.dma_sem is not None else 0.1)
                else:
                    lat = op.lat
                r = f + lat
                if r > rt.get(u_, 0.0):
                    rt[u_] = r
                nd[u_] -= 1
                if nd[u_] == 0:
                    cand.setdefault(u_.eng, []).append(u_)
        self.makespan = max(fin.values())
        return order

    def emit(self, nc, block, sems, dma_sems):
        ops = self.ops
        for op in ops:
            for d in op.deps:
                if d.eng == "pe" and op.eng == "pe":
                    continue
                if d.eng == op.eng and d.dma_sem is None and not SAME_ENGINE_SYNC:
                    continue
                d.need_inc = True
        if LIST_SCHED:
            by_eng = self.list_schedule()
        else:
            by_eng = {}
            for op in ops:
                by_eng.setdefault(op.eng, []).append(op)
        cnt = {}
        for op in ops:
            if op.dma_sem is not None:
                cnt[op.dma_sem] = cnt.get(op.dma_sem, 0) + 16
                op.count = cnt[op.dma_sem]
        for e, lst in by_eng.items():
            for op in lst:
                if op.dma_sem is None and op.need_inc:
                    cnt[e] = cnt.get(e, 0) + 1
                    op.count = cnt[e]

        def run(eng_name, eng):
            waited = {}
            for op in by_eng.get(eng_name, ()):
                need = {}
                for d in op.deps:
                    if d.dma_sem is not None:
                        key = ("d", d.dma_sem)
                    else:
                        if d.eng == "pe" and eng_name == "pe":
                            continue
                        if d.eng == eng_name and not SAME_ENGINE_SYNC:
                            continue
                        key = ("e", d.eng)
                    if d.count > need.get(key, 0):
                        need[key] = d.count
                for key, val in need.items():
                    if val > waited.get(key, 0):
                        sem = dma_sems[key[1]] if key[0] == "d" else sems[key[1]]
                        eng.wait_ge(sem, val)
                        waited[key] = val
                ins = op.fn(eng)
                if op.dma_sem is not None:
                    ins.then_inc(dma_sems[op.dma_sem], 16)
                elif op.need_inc:
                    ins.then_inc(sems[eng_name], 1)

        @block.tensor
        def _(e):
            run("pe", e)

        @block.scalar
        def _(e):
            run("act", e)

        @block.vector
        def _(e):
            run("dve", e)

        @block.gpsimd
        def _(e):
            run("pool", e)

        @block.sync
        def _(e):
            run("sp", e)


def build_program(n_tiles, n_ctx=0, final_norm=True):
    nc = bass.Bass("TRN2", target_bir_lowering=False)
    S = n_tiles * TT
    S_out = (n_tiles - n_ctx) * TT
    NL = 2
    xT = nc.dram_tensor("xT", [D, S], F32, kind="ExternalInput").ap()
    wts = nc.dram_tensor("wts", [NL * UPL * P, 1024], F32, kind="ExternalInput").ap()
    pvec_d = nc.dram_tensor("pvec", [P, NPV], F32, kind="ExternalInput").ap()
    sinks_d = nc.dram_tensor("sinks", [P, 16], F32, kind="ExternalInput").ap()
    gatew_d = nc.dram_tensor("gatew", [P, 8 * P], F32, kind="ExternalInput").ap()
    masks_d = nc.dram_tensor("masks", [P, 512], F32, kind="ExternalInput").ap()
    ident_d = nc.dram_tensor("ident", [P, P], F32, kind="ExternalInput").ap()
    outT = nc.dram_tensor("outT", [D, S_out], F32, kind="ExternalOutput").ap()
    wbf = nc.dram_tensor("wbf", [NL * UPL * P, 1024], BF16).ap()

    xT_v = xT.rearrange("(k p) s -> p k s", p=P)
    outT_v = outT.rearrange("(k p) s -> p k s", p=P)

    sch = Sched()
    es = ExitStack()

    def sb(name, shape, dt):
        return es.enter_context(nc.sbuf_tensor(name, shape, dt))

    def ps(name, shape, dt):
        return es.enter_context(nc.psum_tensor(name, shape, dt))

    with es:
        hbuf = [sb(f"h{i}", [P, KC, TT], F32) for i in range(2)]
        sq = sb("sq", [P, 2, TT], BF16)
        sqf = sb("sqf", [P, 2, TT], BF16)
        hn1 = sb("hn1", [P, KC, TT], BF16)
        hn2 = sb("hn2", [P, KC, TT], BF16)
        rt = sb("rt", [P, TT], F32)
        rstd = sb("rstd", [P, TT], F32)
        rtf = sb("rtf", [P, TT], F32)
        rstdf = sb("rstdf", [P, TT], F32)
        qT = sb("qT", [P, 4, TT], BF16)
        kk = [sb(f"kk{l}", [P, 2, P + TT], BF16) for l in range(2)]
        vtok = [sb(f"vtok{l}", [P, NBLK + 1, P], BF16) for l in range(2)]
        u = [sb(f"u{l}", [P, 2, 30 + TT], F32) for l in range(2)]
        rx = [sb(f"rx{l}", [P, 2, 3 + TT], F32) for l in range(2)]
        hst = [sb(f"hst{l}", [P, 2], F32) for l in range(2)]
        gg = sb("gg", [P, 2, TT], F32)
        cacc = sb("cacc", [P, 2, TT], F32)
        cbf = sb("cbf", [P, 2, TT], BF16)
        csq = sb("csq", [P, 2, TT], BF16)
        mu = sb("mu", [P, TT], F32)
        musq = sb("musq", [P, TT], F32)
        lrs = sb("lrs", [P, TT], F32)
        ysq = sb("ysq", [P, 2, TT], BF16)
        grs = sb("grs", [P, TT], F32)
        xc = sb("xc", [P, 2, TT], F32)
        xcb = sb("xcb", [P, 2, TT], BF16)
        ra = sb("ra", [P, 2, TT], F32)
        a2s = sb("a2s", [P, 2, TT], F32)
        igx = sb("igx", [P, 2, TT], F32)
        hs = sb("hs", [P, 2, TT], F32)
        t2 = a2s
        yc = igx
        sig = hs
        pm = sb("pm", [P, 8, 256], BF16)
        ptsb = [sb(f"ptsb{i}", [P, 8 * P], BF16) for i in range(2)]
        mraw = sb("mraw", [P, 8], F32)
        negm = sb("negm", [P, 2, 8], F32)
        rsum = sb("rsum", [P, 8], F32)
        tsk = sb("tsk", [P, 8], F32)
        esk = sb("esk", [P, 8], F32)
        den = sb("den", [P, 8], F32)
        rden = sb("rden", [P, 8], F32)
        yatt = sb("yatt", [P, 512], F32)
        yasq = sb("yasq", [P, 512], F32)
        ass = sb("ass", [P, 1], F32)
        asd = sb("asd", [P, 1], F32)
        ars = sb("ars", [P, 1], F32)
        yattn = sb("yattn", [P, 512], BF16)
        ymix = sb("ymix", [P, KC, TT], BF16)
        hidden = sb("hidden", [P, 16, TT], BF16)
        r32 = sb("r32", [P, 2, TT], F32)
        ob = sb("ob", [P, 2, TT], F32)
        ringF = sb("ringF", [P, RF, 1024], BF16)
        ringG = sb("ringG", [P, RG, 1024], BF16)
        pvec = sb("pvecs", [P, NPV], F32)
        sinks = sb("sinkss", [P, 16], F32)
        negsinks = sb("negsinks", [P, 16], F32)
        cneg = sb("cneg", [P, 4], F32)
        ctmp = sb("ctmp", [P, 4], F32)
        stage = sb("stage", [P, 8 * P], F32)
        gatew = sb("gatewb", [P, 8 * P], BF16)
        masks = sb("masksb", [P, 768], BF16)
        ident = sb("identb", [P, P], BF16)
        ones = sb("ones", [P, P], BF16)
        psA = [ps(f"psA{i}", [P, 512], F32) for i in range(4)]
        psS2 = ps("psS2", [P, 1024], F32)
        psS = [psS2[:, 0:512], psS2[:, 512:1024]]
        psT = ps("psT", [P, 1024], BF16)
        psO = ps("psO", [P, 512], F32)

        eng_names = ["pe", "act", "dve", "pool"]
        sems = {n: es.enter_context(nc.semaphore(f"sem_{n}")) for n in eng_names}
        n_units = NL * UPL
        gbounds = [0, 1, 3, 6]
        while gbounds[-1] < n_units:
            gbounds.append(min(n_units, gbounds[-1] + GRP))
        NGRP = len(gbounds) - 1
        grp_of = {}
        for g_ in range(NGRP):
            for u_ in range(gbounds[g_], gbounds[g_ + 1]):
                grp_of[u_] = g_
        dma_names = ([f"rF{i}" for i in range(RF)] + [f"rG{i}" for i in range(RG)] + ["xin0", "xin1", "out0", "out1"]
                     + [f"const{i}" for i in range(5)] + [f"pro{g}" for g in range(NGRP)])
        dma_sems = {n: es.enter_context(nc.semaphore(f"dsem_{n}")) for n in dma_names}

        add = sch.add
        ctrF = [0]
        ctrG = [0]

        def bankF():
            i = ctrF[0] % 2
            ctrF[0] += 1
            return i

        def bankG():
            i = 2 + ctrG[0] % 2
            ctrG[0] += 1
            return i

        add("sp", lambda e: e.dma_start(out=pvec[:], in_=pvec_d), writes=["pvec"], dma_sem="const0")
        add("sp", lambda e: e.dma_start(out=sinks[:], in_=sinks_d), writes=["sinks"], dma_sem="const1")
        add("sp", lambda e: e.dma_start(out=stage[:], in_=gatew_d), writes=["stage"], dma_sem="const2")
        add("dve", lambda e: e.tensor_copy(out=gatew[:], in_=stage[:]), reads=["stage"], writes=["gatew"])
        add("sp", lambda e: e.dma_start(out=stage[:, 0:512], in_=masks_d), writes=["stage"], dma_sem="const3")
        add("dve", lambda e: e.tensor_copy(out=masks[:, 0:512], in_=stage[:, 0:512]), reads=["stage"], writes=["masks"])
        add("dve", lambda e: e.tensor_scalar(out=masks[:, 512:640], in0=stage[:, 0:128], scalar1=pvec[:, O_FLAG:O_FLAG + 1],
                                            scalar2=None, op0=ALU.mult), reads=["stage", "pvec"], writes=["masks"])
        add("dve", lambda e: e.tensor_copy(out=masks[:, 640:768], in_=stage[:, 128:256]), reads=["stage"], writes=["masks"])
        add("sp", lambda e: e.dma_start(out=stage[:, 0:P], in_=ident_d), writes=["stage"], dma_sem="const4")
        add("dve", lambda e: e.tensor_copy(out=ident[:], in_=stage[:, 0:P]), reads=["stage"], writes=["ident"])
        add("dve", lambda e: e.memset(ones[:], 1.0), writes=["ones"])
        add("dve", lambda e: e.tensor_scalar(out=negsinks[:], in0=sinks[:], scalar1=-1.0, scalar2=None,
                                            op0=ALU.mult), reads=["sinks"], writes=["negsinks"])
        for l in range(2):
            lam = pvec[:, l * PL + O_LAM: l * PL + O_LAM + 2]
            add("act", lambda e, lam=lam, l=l: e.activation(out=ctmp[:, 2 * l:2 * l + 2], in_=lam, func=AF.Exp,
                                                          scale=-1.0),
                reads=["pvec"], writes=[f"ctmp{l}"])
            add("act", lambda e, l=l: e.activation(out=ctmp[:, 2 * l:2 * l + 2], in_=ctmp[:, 2 * l:2 * l + 2],
                                                 func=AF.Ln, bias=1.0),
                reads=[f"ctmp{l}"], writes=[f"ctmp{l}"])
            add("dve", lambda e, l=l: e.tensor_scalar(out=cneg[:, 2 * l:2 * l + 2], in0=ctmp[:, 2 * l:2 * l + 2],
                                                    scalar1=-8.0, scalar2=None, op0=ALU.mult),
                reads=[f"ctmp{l}"], writes=[f"cneg{l}"])
            add("pool", lambda e, l=l: e.memset(kk[l][:], 0.0), writes=[f"kk{l}_0", f"kk{l}_1"])
            add("pool", lambda e, l=l: e.memset(vtok[l][:], 0.0), writes=[f"vtok{l}"])
            add("pool", lambda e, l=l: e.memset(u[l][:], 0.0), writes=[f"u{l}_0", f"u{l}_1"])
            add("pool", lambda e, l=l: e.memset(rx[l][:], 0.0), writes=[f"rx{l}_0", f"rx{l}_1"])
            add("pool", lambda e, l=l: e.memset(hst[l][:], 0.0), writes=[f"hst{l}_0", f"hst{l}_1"])

        for g in range(NGRP):
            g0 = gbounds[g]
            g1 = gbounds[g + 1]
            add("pool", lambda e, g0=g0, g1=g1: e.dma_start(out=wbf[g0 * P:g1 * P, :], in_=wts[g0 * P:g1 * P, :]),
                writes=[f"wbf{g}", f"prowin{g % 3}"], dma_sem=f"pro{g}", c=1.0, lat=12.0)

        fronts_seq = []
        if n_ctx > 0:
            fronts_seq.append((0, 0, False))
            for t in range(1, n_ctx + 1):
                fronts_seq.append((t, 0, False))
                fronts_seq.append((t - 1, 1, True))
            t = n_ctx
            if t + 1 < n_tiles:
                fronts_seq += [(t + 1, 0, False), (t, 1, False), (t + 1, 1, False)]
                t += 2
            else:
                fronts_seq += [(t, 1, False)]
                t += 1
        else:
            t = 0
        while t < n_tiles:
            pair = [t] if t + 1 >= n_tiles else [t, t + 1]
            for l in range(2):
                for tt_ in pair:
                    fronts_seq.append((tt_, l, False))
            t += 2

        def punits(tt_):
            return range(4, 13) if tt_ == n_ctx - 1 else range(11, 13)

        def f_units():
            for (tt_, l, partial) in fronts_seq:
                for ui in (punits(tt_) if partial else range(23)):
                    yield l * UPL + ui

        def g_units():
            for (tt_, l, partial) in fronts_seq:
                if partial:
                    continue
                for half in range(2):
                    for j in range(16):
                        yield l * UPL + 23 + 16 * half + j
                    for i in range(KC):
                        for q in range(2):
                            yield l * UPL + 55 + i * 4 + 2 * half + q

        class Ring:
            def __init__(self, buf, nslots, tag, seq):
                self.buf, self.n, self.tag, self.seq = buf, nslots, tag, seq
                self.ctr = 0
                self.pending = []

            def _load(self, gu):
                slot = self.ctr % self.n
                self.ctr += 1
                buf, tag = self.buf, self.tag
                add("sp", lambda e, slot=slot, gu=gu, buf=buf: e.dma_start(out=buf[:, slot, :],
                                                                           in_=wbf[gu * P:(gu + 1) * P, :]),
                    reads=[f"wbf{grp_of[gu]}"], writes=[f"{tag}{slot}"], dma_sem=f"{tag}{slot}")
                return slot

            def prefetch(self):
                while len(self.pending) < self.n - 1:
                    try:
                        gu = next(self.seq)
                    except StopIteration:
                        break
                    self.pending.append(self._load(gu))

            def get(self):
                self.prefetch()
                slot = self.pending.pop(0)
                return slot

        RFr = Ring(ringF, RF, "rF", f_units())
        RGr = Ring(ringG, RG, "rG", g_units())

        def norm_stats(hb, sqb, sqtok, rtb, rttok, rsb, rstok, bank):
            h = hbuf[hb]
            for k in range(KC):
                s = k % 2
                add("act", lambda e, k=k, s=s: e.activation(out=sqb[:, s, :], in_=h[:, k, :], func=AF.Square),
                    reads=[f"h{hb}_{k}"], writes=[f"{sqtok}{s}"])
                add("pe", lambda e, k=k, s=s: e.matmul(psA[bank][:], lhsT=ones[:], rhs=sqb[:, s, :],
                                                     start=(k == 0), stop=(k == KC - 1)),
                    reads=[f"{sqtok}{s}", "ones"], writes=[f"psA{bank}"])
            add("act", lambda e: e.activation(out=rtb[:], in_=psA[bank][:], func=AF.Ln, scale=1.0 / D, bias=1e-6),
                reads=[f"psA{bank}"], writes=[rttok])
            add("act", lambda e: e.activation(out=rsb[:], in_=rtb[:], func=AF.Exp, scale=-0.5), reads=[rttok], writes=[rstok])

        def merge_gen(gens, totals, scale):
            acc = [0.0] * len(gens)
            alive = [True] * len(gens)
            while any(alive):
                i = min((acc[j] / totals[j], j) for j in range(len(gens)) if alive[j])[1]
                try:
                    c = next(gens[i])
                    acc[i] += c
                    yield c * scale
                except StopIteration:
                    alive[i] = False

        def front(t, l, partial, mk0):
            hb = t % 2
            h = hbuf[hb]
            pb = l * PL
            if l == 0:
                add("sp", lambda e: e.dma_start(out=h[:], in_=xT_v[:, :, t * TT:(t + 1) * TT]),
                    writes=[f"h{hb}_{k}" for k in range(KC)], dma_sem=f"xin{hb}", lat=12.0)
            norm_stats(hb, sq, "sq", rt, "rt", rstd, "rstd", bankF())
            yield 6.0
            for k in range(KC):
                add("dve", lambda e, k=k: e.scalar_tensor_tensor(out=hn1[:, k, :], in0=h[:, k, :],
                                                               scalar=pvec[:, pb + O_G1 + k:pb + O_G1 + k + 1],
                                                               in1=rstd[:], op0=ALU.mult, op1=ALU.mult),
                    reads=[f"h{hb}_{k}", "rstd", "pvec"], writes=[f"hn1_{k}"])
            yield 5.0
            hn_all = [f"hn1_{k}" for k in range(KC)]
            for j in (punits(t) if partial else range(15)):
                slot = RFr.get()
                bank = bankF()
                if j == 6:
                    for blk in range(NBLK):
                        for k in range(KC):
                            add("pe", lambda e, blk=blk, k=k, slot=slot, bank=bank: e.matmul(
                                psA[bank][:, blk * P:(blk + 1) * P], lhsT=hn1[:, k, blk * P:(blk + 1) * P],
                                rhs=ringF[:, slot, k * P:(k + 1) * P], start=(k == 0), stop=(k == KC - 1)),
                                reads=[f"rF{slot}"] + hn_all, writes=[f"psA{bank}"], c=(0.15 if j == 6 else 0.25))
                    add("act", lambda e, bank=bank: e.activation(
                        out=vtok[l][:, 1:NBLK + 1, :],
                        in_=psA[bank][:].rearrange("p (b c) -> p b c", b=NBLK), func=AF.Copy),
                        reads=[f"psA{bank}"], writes=[f"vtok{l}"])
                    yield 0.0
                    continue
                for k in range(KC):
                    add("pe", lambda e, k=k, slot=slot, bank=bank: e.matmul(
                        psA[bank][:], lhsT=ringF[:, slot, k * P:(k + 1) * P], rhs=hn1[:, k, :],
                        start=(k == 0), stop=(k == KC - 1)),
                        reads=[f"rF{slot}"] + hn_all, writes=[f"psA{bank}"], c=(0.15 if j == 6 else 0.25))
                if j < 4:
                    add("act", lambda e, j=j, bank=bank: e.activation(out=qT[:, j, :], in_=psA[bank][:],
                                                                    func=AF.Copy, scale=0.125),
                        reads=[f"psA{bank}"], writes=[f"qT{j}"])
                elif j < 6:
                    c = j - 4
                    add("act", lambda e, c=c, bank=bank: e.activation(out=kk[l][:, c, P:P + TT], in_=psA[bank][:],
                                                                    func=AF.Copy),
                        reads=[f"psA{bank}"], writes=[f"kk{l}_{c}"])
                elif j < 9:
                    c = j - 7
                    add("act", lambda e, c=c, bank=bank: e.activation(out=sig[:, c, :], in_=psA[bank][:],
                                                                    func=AF.Sigmoid),
                        reads=[f"psA{bank}"], writes=[f"hs{c}"])
                elif j < 11:
                    c = j - 9
                    add("dve", lambda e, c=c, bank=bank: e.tensor_tensor(out=u[l][:, c, 30:30 + TT], in0=psA[bank][:],
                                                                       in1=sig[:, c, :], op=ALU.mult),
                        reads=[f"psA{bank}", f"hs{c}"], writes=[f"u{l}_{c}"])
                elif j < 13:
                    c = j - 11
                    add("act", lambda e, c=c, bank=bank: e.activation(out=rx[l][:, c, 3:3 + TT], in_=psA[bank][:],
                                                                    func=AF.Copy),
                        reads=[f"psA{bank}"], writes=[f"rx{l}_{c}"])
                else:
                    c = j - 13
                    add("act", lambda e, c=c, bank=bank: e.activation(out=gg[:, c, :], in_=psA[bank][:],
                                                                    func=GELU),
                        reads=[f"psA{bank}"], writes=[f"gg{c}"])
                yield 0.0

            def attention():
                def stA(b, qd):
                    mk = mk0 if b == 0 else 0
                    bp = b % 2
                    for e2 in range(2):
                        for i in range(2):
                            c = 2 * qd + i
                            add("pe", lambda e, e2=e2, i=i, c=c: e.matmul(
                                psS[e2][:, i * 256:(i + 1) * 256], lhsT=qT[64 * e2:64 * e2 + 64, c, b * P:(b + 1) * P],
                                rhs=kk[l][64 * e2:64 * e2 + 64, qd, b * P:b * P + 256], start=True, stop=True),
                                reads=[f"qT{c}", f"kk{l}_{qd}"], writes=[f"psS{e2}"], c=0.17)
                    add("dve", lambda e: e.reduce_max(
                        out=mraw[:, 4 * qd:4 * qd + 4],
                        in_=psS2[:, :].rearrange("p (x k) -> p x k", x=4), axis=AX.X),
                        reads=["psS0", "psS1"], writes=[f"mrawq{qd}"], c=1.1)
                    add("dve", lambda e: e.scalar_tensor_tensor(
                        out=negm[:, bp, 4 * qd:4 * qd + 4], in0=mraw[:, 4 * qd:4 * qd + 4], scalar=-1.0,
                        in1=negsinks[:, 8 * l + 4 * qd:8 * l + 4 * qd + 4], op0=ALU.mult, op1=ALU.min),
                        reads=[f"mrawq{qd}", "negsinks"], writes=[f"negm{bp}q{qd}"], c=0.15)
                    for x in range(4):
                        pos = 4 * qd + x
                        e2 = x // 2
                        add("act", lambda e, x=x, pos=pos, e2=e2: e.activation(
                            out=pm[:, pos, :], in_=psS2[:, x * 256:(x + 1) * 256], func=AF.Exp,
                            bias=negm[:, bp, pos:pos + 1]),
                            reads=[f"psS{e2}", f"negm{bp}q{qd}"], writes=[f"pmq{qd}"], c=0.45)
                    add("pool", lambda e: e.tensor_tensor(
                        out=pm[:, 4 * qd:4 * qd + 4, :], in0=pm[:, 4 * qd:4 * qd + 4, :],
                        in1=masks[:, mk * 256:(mk + 1) * 256].unsqueeze(1).to_broadcast([P, 4, 256]), op=ALU.mult),
                        reads=[f"pmq{qd}", "masks"], writes=[f"pmq{qd}"], c=1.8)

                def stB(b, qd):
                    th = qd % 2
                    for x in range(4):
                        pos = 4 * qd + x
                        for half in range(2):
                            col = (2 * x + half) * P
                            add("pe", lambda e, pos=pos, half=half, col=col: e.transpose(
                                out=psT[:, col:col + P], in_=pm[:, pos, half * P:(half + 1) * P], identity=ident[:]),
                                reads=[f"pmq{qd}", "ident"], writes=["psT"], c=0.15)
                    add("act", lambda e: e.activation(out=ptsb[th][:], in_=psT[:, :], func=AF.Copy),
                        reads=["psT"], writes=[f"ptsb{th}"], c=1.1)
                    for x in range(4):
                        e2, i = x // 2, x % 2
                        hh = 4 * qd + 2 * i + e2
                        for half in range(2):
                            add("pe", lambda e, x=x, hh=hh, half=half: e.matmul(
                                psO[:, hh * 64:(hh + 1) * 64],
                                lhsT=ptsb[th][:, (2 * x + half) * P:(2 * x + half + 1) * P],
                                rhs=vtok[l][:, b + half, qd * 64:(qd + 1) * 64], start=(half == 0), stop=(half == 1)),
                                reads=[f"ptsb{th}", f"vtok{l}"], writes=["psO"], c=0.1)
                    add("dve", lambda e: e.reduce_sum(out=rsum[:, 4 * qd:4 * qd + 4], in_=pm[:, 4 * qd:4 * qd + 4, :],
                                                     axis=AX.X),
                        reads=[f"pmq{qd}"], writes=[f"rsumq{qd}"], c=1.1)

                def stT1(b):
                    bp = b % 2
                    add("dve", lambda e: e.tensor_tensor(out=tsk[:], in0=sinks[:, 8 * l:8 * l + 8], in1=negm[:, bp, :],
                                                       op=ALU.add),
                        reads=["sinks", f"negm{bp}q0", f"negm{bp}q1"], writes=["tsk"], c=0.15)
                    add("act", lambda e: e.activation(out=esk[:], in_=tsk[:], func=AF.Exp), reads=["tsk"],
                        writes=["esk"], c=0.2)
                    add("dve", lambda e: e.tensor_tensor(out=den[:], in0=rsum[:], in1=esk[:], op=ALU.add),
                        reads=["rsumq0", "rsumq1", "esk"], writes=["den"], c=0.15)
                    add("dve", lambda e: e.reciprocal(
                        out=rden[:].rearrange("p (q i e) -> p q e i", q=2, i=2, e=2),
                        in_=den[:].rearrange("p (q e i) -> p q e i", q=2, e=2, i=2)),
                        reads=["den"], writes=["rden"], c=0.15)
                    add("dve", lambda e: e.tensor_tensor(
                        out=yatt[:].rearrange("p (h d) -> p h d", h=8), in0=psO[:].rearrange("p (h d) -> p h d", h=8),
                        in1=rden[:].unsqueeze(2).to_broadcast([P, 8, 64]), op=ALU.mult),
                        reads=["psO", "rden"], writes=["yatt"])

                def stT2(b):
                    add("pool", lambda e: e.tensor_tensor(out=yasq[:], in0=yatt[:], in1=yatt[:], op=ALU.mult),
                        reads=["yatt"], writes=["yasq"])
                    add("dve", lambda e: e.reduce_sum(out=ass[:], in_=yasq[:], axis=AX.X), reads=["yasq"],
                        writes=["ass"])
                    add("act", lambda e: e.activation(out=asd[:], in_=ass[:], func=AF.Ln, scale=1.0 / 512,
                                                      bias=1e-6),
                        reads=["ass"], writes=["asd"], c=0.15)
                    add("act", lambda e: e.activation(out=ars[:], in_=asd[:], func=AF.Exp, scale=-0.5), reads=["asd"], writes=["ars"], c=0.15)
                    add("act", lambda e: e.activation(out=yattn[:], in_=yatt[:], func=AF.Copy, scale=ars[:, 0:1]),
                        reads=["yatt", "ars"], writes=["yattn"])

                def stT3(b):
                    for cc in range(4):
                        add("pe", lambda e, cc=cc: e.transpose(out=psT[:, cc * P:(cc + 1) * P],
                                                             in_=yattn[:, cc * P:(cc + 1) * P], identity=ident[:]),
                            reads=["yattn", "ident"], writes=["psT"], c=0.15)
                    for cc in range(4):
                        add("act", lambda e, cc=cc: e.activation(
                            out=ymix[:, cc, b * P:(b + 1) * P], in_=psT[:, cc * P:(cc + 1) * P], func=AF.Copy,
                            scale=pvec[:, pb + O_MG + cc:pb + O_MG + cc + 1]),
                            reads=["psT", "pvec"], writes=[f"ymix{cc}"], c=0.3)

                NQ = 2 * NBLK
                tails = []
                stA(0, 0)
                yield 3.0
                for p in range(NQ):
                    if p + 1 < NQ:
                        stA((p + 1) // 2, (p + 1) % 2)
                        yield 3.0
                    if tails:
                        fn, cst = tails.pop(0)
                        fn()
                        yield cst
                    stB(p // 2, p % 2)
                    yield 3.0
                    if p % 2 == 1:
                        bb = p // 2
                        tails += [(lambda bb=bb: stT1(bb), 2.0), (lambda bb=bb: stT2(bb), 2.5),
                                  (lambda bb=bb: stT3(bb), 1.0)]
                    if tails:
                        fn, cst = tails.pop(0)
                        fn()
                        yield cst
                while tails:
                    fn, cst = tails.pop(0)
                    fn()
                    yield cst

            def lru_branch():
                for c in range(2):
                    wb = pb + O_LW + 4 * c
                    add("dve", lambda e, c=c, wb=wb: e.tensor_scalar(
                        out=xc[:, c, :], in0=rx[l][:, c, 0:TT], scalar1=pvec[:, wb:wb + 1],
                        scalar2=pvec[:, pb + O_LB + c:pb + O_LB + c + 1], op0=ALU.mult, op1=ALU.add),
                        reads=[f"rx{l}_{c}", "pvec"], writes=[f"xc{c}"])
                    for kq in range(1, 4):
                        add("dve", lambda e, c=c, wb=wb, kq=kq: e.scalar_tensor_tensor(
                            out=xc[:, c, :], in0=rx[l][:, c, kq:kq + TT], scalar=pvec[:, wb + kq:wb + kq + 1],
                            in1=xc[:, c, :], op0=ALU.mult, op1=ALU.add),
                            reads=[f"rx{l}_{c}", "pvec", f"xc{c}"], writes=[f"xc{c}"])
                    add("pool", lambda e, c=c: e.tensor_copy(out=rx[l][:, c, 0:3], in_=rx[l][:, c, TT:TT + 3]),
                        reads=[f"rx{l}_{c}"], writes=[f"rx{l}_{c}"])
                    add("pool", lambda e, c=c: e.tensor_copy(out=xcb[:, c, :], in_=xc[:, c, :]),
                        reads=[f"xc{c}"], writes=[f"xcb{c}"])
                    yield 3.0
                    for gate in range(2):
                        bank = bankF()
                        gi = (l * 2 + gate) * 2 + c
                        add("pe", lambda e, c=c, gi=gi, bank=bank: e.matmul(
                            psA[bank][:], lhsT=gatew[:, gi * P:(gi + 1) * P], rhs=xcb[:, c, :], start=True, stop=True),
                            reads=[f"xcb{c}", "gatew"], writes=[f"psA{bank}"])
                        dst = ra if gate == 0 else igx
                        dtok = f"ra{c}" if gate == 0 else f"igx{c}"
                        bo = pb + (O_BA if gate == 0 else O_BX) + c
                        add("act", lambda e, c=c, dst=dst, bo=bo, bank=bank: e.activation(
                            out=dst[:, c, :], in_=psA[bank][:], func=AF.Sigmoid, bias=pvec[:, bo:bo + 1]),
                            reads=[f"psA{bank}", "pvec"], writes=[dtok])
                    add("act", lambda e, c=c: e.activation(out=ra[:, c, :], in_=ra[:, c, :], func=AF.Exp,
                                                         scale=cneg[:, 2 * l + c:2 * l + c + 1]),
                        reads=[f"ra{c}", f"cneg{l}"], writes=[f"ra{c}"])
                    add("pool", lambda e, c=c: e.tensor_tensor(out=a2s[:, c, :], in0=ra[:, c, :], in1=ra[:, c, :],
                                                             op=ALU.mult),
                        reads=[f"ra{c}"], writes=[f"a2s{c}"])
                    add("act", lambda e, c=c: e.activation(out=a2s[:, c, :], in_=a2s[:, c, :], func=AF.Ln,
                                                         scale=-1.0, bias=1.0),
                        reads=[f"a2s{c}"], writes=[f"a2s{c}"])
                    add("act", lambda e, c=c: e.activation(out=a2s[:, c, :], in_=a2s[:, c, :], func=AF.Exp, scale=0.5),
                        reads=[f"a2s{c}"], writes=[f"a2s{c}"])
                    yield 2.0
                    add("pool", lambda e, c=c: e.tensor_tensor(out=igx[:, c, :], in0=igx[:, c, :], in1=xc[:, c, :],
                                                             op=ALU.mult),
                        reads=[f"igx{c}", f"xc{c}"], writes=[f"igx{c}"])
                    add("pool", lambda e, c=c: e.tensor_tensor(out=igx[:, c, :], in0=igx[:, c, :], in1=a2s[:, c, :],
                                                             op=ALU.mult),
                        reads=[f"igx{c}", f"a2s{c}"], writes=[f"igx{c}"])
                    add("dve", lambda e, c=c: e.tensor_tensor_scan(out=hs[:, c, :], data0=ra[:, c, :],
                                                                 data1=igx[:, c, :], initial=hst[l][:, c:c + 1],
                                                                 op0=ALU.mult, op1=ALU.add),
                        reads=[f"ra{c}", f"igx{c}", f"hst{l}_{c}"], writes=[f"hs{c}"], c=1.3)
                    add("pool", lambda e, c=c: e.tensor_copy(out=hst[l][:, c:c + 1], in_=hs[:, c, TT - 1:TT]),
                        reads=[f"hs{c}"], writes=[f"hst{l}_{c}"], c=0.2)
                    if partial:
                        yield 2.0
                        continue
                    add("pool", lambda e, c=c: e.tensor_tensor(out=gg[:, c, :], in0=hs[:, c, :], in1=gg[:, c, :],
                                                             op=ALU.mult),
                        reads=[f"hs{c}", f"gg{c}"], writes=[f"gg{c}"])
                    add("act", lambda e, c=c: e.activation(out=ysq[:, c, :], in_=gg[:, c, :], func=AF.Square),
                        reads=[f"gg{c}"], writes=[f"ysq{c}"])
                    yield 3.0
                if partial:
                    return
                bank = bankF()
                for c in range(2):
                    add("pe", lambda e, c=c, bank=bank: e.matmul(psA[bank][:], lhsT=ones[:], rhs=ysq[:, c, :],
                                                               start=(c == 0), stop=(c == 1)),
                        reads=[f"ysq{c}", "ones"], writes=[f"psA{bank}"])
                add("act", lambda e, bank=bank: e.activation(out=grs[:], in_=psA[bank][:], func=AF.Ln,
                                                           scale=1.0 / 256, bias=1e-6),
                    reads=[f"psA{bank}"], writes=["grs"])
                add("act", lambda e: e.activation(out=grs[:], in_=grs[:], func=AF.Exp, scale=-0.5), reads=["grs"], writes=["grs"])
                for c in range(2):
                    add("dve", lambda e, c=c: e.scalar_tensor_tensor(
                        out=ymix[:, 6 + c, :], in0=gg[:, c, :], scalar=pvec[:, pb + O_MG + 6 + c:pb + O_MG + 7 + c],
                        in1=grs[:], op0=ALU.mult, op1=ALU.mult),
                        reads=[f"gg{c}", "grs", "pvec"], writes=[f"ymix{6 + c}"])
                yield 2.5

            def conv_taps():
                for kq in range(31):
                    for c in range(2):
                        wcol = pb + O_CW + c * 31 + kq
                        if kq == 0:
                            add("dve", lambda e, c=c, wcol=wcol: e.tensor_scalar(
                                out=cacc[:, c, :], in0=u[l][:, c, 0:TT], scalar1=pvec[:, wcol:wcol + 1],
                                scalar2=pvec[:, pb + O_CB + c:pb + O_CB + c + 1], op0=ALU.mult, op1=ALU.add),
                                reads=[f"u{l}_{c}", "pvec"], writes=[f"cacc{c}"])
                        else:
                            add("dve", lambda e, c=c, wcol=wcol, kq=kq: e.scalar_tensor_tensor(
                                out=cacc[:, c, :], in0=u[l][:, c, kq:kq + TT], scalar=pvec[:, wcol:wcol + 1],
                                in1=cacc[:, c, :], op0=ALU.mult, op1=ALU.add),
                                reads=[f"u{l}_{c}", "pvec", f"cacc{c}"], writes=[f"cacc{c}"])
                    yield 1.25

            def conv_rest():
                for c in range(2):
                    add("pool", lambda e, c=c: e.tensor_copy(out=u[l][:, c, 0:30], in_=u[l][:, c, TT:TT + 30]),
                        reads=[f"u{l}_{c}"], writes=[f"u{l}_{c}"])
                    add("pool", lambda e, c=c: e.tensor_copy(out=cbf[:, c, :], in_=cacc[:, c, :]),
                        reads=[f"cacc{c}"], writes=[f"cbf{c}"])
                    add("act", lambda e, c=c: e.activation(out=csq[:, c, :], in_=cacc[:, c, :], func=AF.Square),
                        reads=[f"cacc{c}"], writes=[f"csq{c}"])
                yield 1.5
                b1 = bankF()
                b2 = bankF()
                for c in range(2):
                    add("pe", lambda e, c=c: e.matmul(psA[b1][:], lhsT=ones[:], rhs=cbf[:, c, :],
                                                    start=(c == 0), stop=(c == 1)),
                        reads=[f"cbf{c}", "ones"], writes=[f"psA{b1}"])
                for c in range(2):
                    add("pe", lambda e, c=c: e.matmul(psA[b2][:], lhsT=ones[:], rhs=csq[:, c, :],
                                                    start=(c == 0), stop=(c == 1)),
                        reads=[f"csq{c}", "ones"], writes=[f"psA{b2}"])
                add("act", lambda e: e.activation(out=mu[:], in_=psA[b1][:], func=AF.Copy, scale=1.0 / 256),
                    reads=[f"psA{b1}"], writes=["mu"])
                add("dve", lambda e: e.tensor_tensor(out=musq[:], in0=mu[:], in1=mu[:], op=ALU.mult),
                    reads=["mu"], writes=["musq"])
                add("dve", lambda e: e.scalar_tensor_tensor(out=lrs[:], in0=psA[b2][:], scalar=1.0 / 256,
                                                          in1=musq[:], op0=ALU.mult, op1=ALU.subtract),
                    reads=[f"psA{b2}", "musq"], writes=["lrs"])
                add("act", lambda e: e.activation(out=lrs[:], in_=lrs[:], func=AF.Ln, bias=1e-5),
                    reads=["lrs"], writes=["lrs"])
                add("act", lambda e: e.activation(out=lrs[:], in_=lrs[:], func=AF.Exp, scale=-0.5), reads=["lrs"], writes=["lrs"])
                yield 3.0
                for c in range(2):
                    add("pool", lambda e, c=c: e.tensor_tensor(out=t2[:, c, :], in0=cacc[:, c, :], in1=mu[:],
                                                             op=ALU.subtract),
                        reads=[f"cacc{c}", "mu"], writes=[f"a2s{c}"])
                    add("pool", lambda e, c=c: e.tensor_tensor(out=t2[:, c, :], in0=t2[:, c, :], in1=lrs[:],
                                                             op=ALU.mult),
                        reads=[f"a2s{c}", "lrs"], writes=[f"a2s{c}"])
                    add("act", lambda e, c=c: e.activation(
                        out=yc[:, c, :], in_=t2[:, c, :], func=AF.Silu,
                        scale=pvec[:, pb + O_LNG + c:pb + O_LNG + c + 1],
                        bias=pvec[:, pb + O_LNB + c:pb + O_LNB + c + 1]),
                        reads=[f"a2s{c}", "pvec"], writes=[f"igx{c}"])
                    add("act", lambda e, c=c: e.activation(out=ysq[:, c, :], in_=yc[:, c, :], func=AF.Square),
                        reads=[f"igx{c}"], writes=[f"ysq{c}"])
                yield 3.0
                bank = bankF()
                for c in range(2):
                    add("pe", lambda e, c=c, bank=bank: e.matmul(psA[bank][:], lhsT=ones[:], rhs=ysq[:, c, :],
                                                               start=(c == 0), stop=(c == 1)),
                        reads=[f"ysq{c}", "ones"], writes=[f"psA{bank}"])
                add("act", lambda e, bank=bank: e.activation(out=grs[:], in_=psA[bank][:], func=AF.Ln,
                                                           scale=1.0 / 256, bias=1e-6),
                    reads=[f"psA{bank}"], writes=["grs"])
                add("act", lambda e: e.activation(out=grs[:], in_=grs[:], func=AF.Exp, scale=-0.5), reads=["grs"], writes=["grs"])
                for c in range(2):
                    add("dve", lambda e, c=c: e.scalar_tensor_tensor(
                        out=ymix[:, 4 + c, :], in0=yc[:, c, :], scalar=pvec[:, pb + O_MG + 4 + c:pb + O_MG + 5 + c],
                        in1=grs[:], op0=ALU.mult, op1=ALU.mult),
                        reads=[f"igx{c}", "grs", "pvec"], writes=[f"ymix{4 + c}"])
                yield 2.5

            if partial:
                yield from lru_branch()
                if t != n_ctx - 1:
                    return
                for c in range(2):
                    add("pool", lambda e, c=c: e.tensor_copy(out=u[l][:, c, 0:30], in_=u[l][:, c, TT:TT + 30]),
                        reads=[f"u{l}_{c}"], writes=[f"u{l}_{c}"])
            else:
                yield from merge_gen([lru_branch(), conv_taps(), attention()], [18.5, 39.0, 70.0], 0.75)
                yield from conv_rest()
            for g in range(2):
                add("pool", lambda e, g=g: e.tensor_copy(out=kk[l][:, g, 0:P], in_=kk[l][:, g, TT:TT + P]),
                    reads=[f"kk{l}_{g}"], writes=[f"kk{l}_{g}"])
            add("pool", lambda e: e.tensor_copy(out=vtok[l][:, 0, :], in_=vtok[l][:, NBLK, :]),
                reads=[f"vtok{l}"], writes=[f"vtok{l}"])

            if partial:
                return
            ymix_all = [f"ymix{k}" for k in range(KC)]
            for i in range(KC):
                slot = RFr.get()
                bank = bankF()
                for k in range(KC):
                    add("pe", lambda e, k=k, slot=slot, bank=bank: e.matmul(
                        psA[bank][:], lhsT=ringF[:, slot, k * P:(k + 1) * P], rhs=ymix[:, k, :],
                        start=(k == 0), stop=(k == KC - 1)),
                        reads=[f"rF{slot}"] + ymix_all, writes=[f"psA{bank}"])
                add("dve", lambda e, i=i, bank=bank: e.tensor_tensor(out=h[:, i, :], in0=psA[bank][:], in1=h[:, i, :],
                                                                   op=ALU.add),
                    reads=[f"psA{bank}", f"h{hb}_{i}"], writes=[f"h{hb}_{i}"])
                yield 0.0
            norm_stats(hb, sqf, "sqf", rtf, "rtf", rstdf, "rstdf", bankF())
            yield 6.0
            for k in range(KC):
                add("dve", lambda e, k=k: e.scalar_tensor_tensor(out=hn2[:, k, :], in0=h[:, k, :],
                                                               scalar=pvec[:, pb + O_G2 + k:pb + O_G2 + k + 1],
                                                               in1=rstdf[:], op0=ALU.mult, op1=ALU.mult),
                    reads=[f"h{hb}_{k}", "rstdf", "pvec"], writes=[f"hn2_{k}"])
            yield 5.0

        def ffn(t, l, do_final):
            hb = t % 2
            h = hbuf[hb]
            hn_all = [f"hn2_{k}" for k in range(KC)]
            for half in range(2):
                for jj in range(16):
                    slot = RGr.get()
                    bank = bankG()
                    for k in range(KC):
                        add("pe", lambda e, k=k, slot=slot, bank=bank: e.matmul(
                            psA[bank][:], lhsT=ringG[:, slot, k * P:(k + 1) * P], rhs=hn2[:, k, :],
                            start=(k == 0), stop=(k == KC - 1)),
                            reads=[f"rG{slot}"] + hn_all, writes=[f"psA{bank}"])
                    rs_ = jj % 2
                    add("act", lambda e, rs_=rs_, bank=bank: e.activation(out=r32[:, rs_, :], in_=psA[bank][:],
                                                                        func=AF.Relu),
                        reads=[f"psA{bank}"], writes=[f"r32{rs_}"])
                    add("act", lambda e, rs_=rs_, jj=jj: e.activation(out=hidden[:, jj, :], in_=r32[:, rs_, :],
                                                                    func=AF.Square),
                        reads=[f"r32{rs_}"], writes=[f"hid{jj}"])
                    yield 1.95
                for i in range(KC):
                    bank = bankG()
                    for q in range(2):
                        slot = RGr.get()
                        for k2 in range(8):
                            jj = 8 * q + k2
                            add("pe", lambda e, k2=k2, jj=jj, q=q, slot=slot, bank=bank: e.matmul(
                                psA[bank][:], lhsT=ringG[:, slot, k2 * P:(k2 + 1) * P], rhs=hidden[:, jj, :],
                                start=(q == 0 and k2 == 0), stop=(q == 1 and k2 == 7)),
                                reads=[f"rG{slot}", f"hid{jj}"], writes=[f"psA{bank}"])
                        yield 1.95
                    add("dve", lambda e, i=i, bank=bank: e.tensor_tensor(out=h[:, i, :], in0=psA[bank][:],
                                                                       in1=h[:, i, :], op=ALU.add),
                        reads=[f"psA{bank}", f"h{hb}_{i}"], writes=[f"h{hb}_{i}"])
            if do_final:
                norm_stats(hb, sqf, "sqf", rtf, "rtf", rstdf, "rstdf", bankG())
                yield 4.0
                gb = 2 * PL
                for k in range(KC):
                    o = k % 2
                    add("dve", lambda e, k=k, o=o: e.scalar_tensor_tensor(
                        out=ob[:, o, :], in0=h[:, k, :], scalar=pvec[:, gb + k:gb + k + 1], in1=rstdf[:],
                        op0=ALU.mult, op1=ALU.mult),
                        reads=[f"h{hb}_{k}", "rstdf", "pvec"], writes=[f"ob{o}"])
                    add("sp", lambda e, k=k, o=o: e.dma_start(out=outT_v[:, k, (t - n_ctx) * TT:(t - n_ctx + 1) * TT],
                                                            in_=ob[:, o, :]),
                        reads=[f"ob{o}"], writes=[f"outdram{o}"], dma_sem=f"out{o}")
                yield 4.0

        ffn_q = []
        ffn_done = set()

        def ffn_step():
            key, gen = ffn_q[0]
            sch.prio_bias = FFN_PRIO_BIAS
            try:
                return next(gen)
            except StopIteration:
                ffn_done.add(key)
                ffn_q.pop(0)
                return 0.0
            finally:
                sch.prio_bias = 0

        def drain_until(key):
            while key not in ffn_done:
                assert ffn_q, key
                ffn_step()

        def run_front(gB):
            tA = tB = 0.0
            while True:
                if ffn_q and tA < tB:
                    tA += ffn_step()
                else:
                    try:
                        tB += next(gB)
                    except StopIteration:
                        break

        def reset_states(l):
            fl = pvec[:, O_FLAG:O_FLAG + 1]
            for g in range(2):
                add("pool", lambda e, g=g: e.tensor_scalar(out=kk[l][:, g, 0:P], in0=kk[l][:, g, 0:P], scalar1=fl,
                                                         scalar2=None, op0=ALU.mult),
                    reads=[f"kk{l}_{g}", "pvec"], writes=[f"kk{l}_{g}"])
                add("pool", lambda e, g=g: e.tensor_scalar(out=u[l][:, g, 0:30], in0=u[l][:, g, 0:30], scalar1=fl,
                                                         scalar2=None, op0=ALU.mult),
                    reads=[f"u{l}_{g}", "pvec"], writes=[f"u{l}_{g}"])
                add("pool", lambda e, g=g: e.tensor_scalar(out=rx[l][:, g, 0:3], in0=rx[l][:, g, 0:3], scalar1=fl,
                                                         scalar2=None, op0=ALU.mult),
                    reads=[f"rx{l}_{g}", "pvec"], writes=[f"rx{l}_{g}"])
                add("pool", lambda e, g=g: e.tensor_scalar(out=hst[l][:, g:g + 1], in0=hst[l][:, g:g + 1], scalar1=fl,
                                                         scalar2=None, op0=ALU.mult),
                    reads=[f"hst{l}_{g}", "pvec"], writes=[f"hst{l}_{g}"])
            add("pool", lambda e: e.tensor_scalar(out=vtok[l][:, 0, :], in0=vtok[l][:, 0, :], scalar1=fl,
                                                scalar2=None, op0=ALU.mult),
                reads=[f"vtok{l}", "pvec"], writes=[f"vtok{l}"])

        for (t, l, partial) in fronts_seq:
            if l == 1:
                drain_until((t, 0))
            elif t >= 2 and (t - 2) >= n_ctx:
                drain_until((t - 2, 1))
            if n_ctx > 0 and t == n_ctx:
                reset_states(l)
            mk0 = 1 if t == 0 else (2 if (n_ctx > 0 and t == n_ctx) else 0)
            run_front(front(t, l, partial, mk0))
            if not partial:
                ffn_q.append(((t, l), ffn(t, l, do_final=(l == 1 and final_norm))))
        while ffn_q:
            ffn_step()
        add("sp", lambda e: e.nop(), reads=["outdram0", "outdram1"], writes=["done"])

        with nc.Block() as block:
            sch.emit(nc, block, sems, dma_sems)
    return nc


def _unit(mat_rows_by_cols):
    return np.ascontiguousarray(mat_rows_by_cols.reshape(8, P, P).transpose(1, 0, 2)).reshape(P, 1024)


def prep_weights(w_in, w_out, w_up, w_down, layers):
    out = np.empty((len(layers), UPL, P, 1024), np.float32)
    for li, l in enumerate(layers):
        wi = w_in[l]
        col_sets = []
        for j in range(4):
            col_sets.append(np.arange(j * 128, (j + 1) * 128))
        k0 = np.arange(512, 576)
        k1 = np.arange(576, 640)
        col_sets.append(np.concatenate([k0, k0]))
        col_sets.append(np.concatenate([k1, k1]))
        col_sets.append(np.arange(640, 768))
        col_sets += [np.arange(1024 + c * 128, 1024 + (c + 1) * 128) for c in range(2)]
        col_sets += [np.arange(768 + c * 128, 768 + (c + 1) * 128) for c in range(2)]
        col_sets += [np.arange(1280 + c * 128, 1280 + (c + 1) * 128) for c in range(2)]
        col_sets += [np.arange(1536 + c * 128, 1536 + (c + 1) * 128) for c in range(2)]
        ui = 0
        for cs in col_sets:
            out[li, ui] = _unit(wi[:, cs])
            ui += 1
        for i in range(8):
            out[li, ui] = _unit(w_out[l][:, i * 128:(i + 1) * 128])
            ui += 1
        for j in range(32):
            out[li, ui] = _unit(w_up[l][:, j * 128:(j + 1) * 128])
            ui += 1
        for i in range(8):
            for q in range(4):
                out[li, ui] = _unit(w_down[l][q * 1024:(q + 1) * 1024, i * 128:(i + 1) * 128])
                ui += 1
        assert ui == UPL
    return out.reshape(len(layers) * UPL * P, 1024)


def prep_small(inp):
    pv = np.zeros((P, NPV), np.float32)
    for l in range(2):
        b = l * PL
        pv[:, b + O_G1:b + O_G1 + 8] = inp["norm1"][l].reshape(8, P).T
        pv[:, b + O_G2:b + O_G2 + 8] = inp["norm2"][l].reshape(8, P).T
        pv[:, b + O_MG:b + O_MG + 8] = inp["mix_norm"][l].reshape(8, P).T
        for c in range(2):
            pv[:, b + O_CW + c * 31:b + O_CW + (c + 1) * 31] = inp["conv_dw_w"][l][:, c * P:(c + 1) * P].T
            pv[:, b + O_LW + c * 4:b + O_LW + (c + 1) * 4] = inp["lru_conv_w"][l][:, c * P:(c + 1) * P].T
        pv[:, b + O_CB:b + O_CB + 2] = inp["conv_dw_b"][l].reshape(2, P).T
        pv[:, b + O_LNG:b + O_LNG + 2] = inp["conv_ln_g"][l].reshape(2, P).T
        pv[:, b + O_LNB:b + O_LNB + 2] = inp["conv_ln_b"][l].reshape(2, P).T
        pv[:, b + O_LB:b + O_LB + 2] = inp["lru_conv_b"][l].reshape(2, P).T
        pv[:, b + O_BA:b + O_BA + 2] = inp["lru_ba"][l].reshape(2, P).T
        pv[:, b + O_BX:b + O_BX + 2] = inp["lru_bx"][l].reshape(2, P).T
        pv[:, b + O_LAM:b + O_LAM + 2] = inp["lru_lambda"][l].reshape(2, P).T
    pv[:, 2 * PL:2 * PL + 8] = inp["final_norm"].reshape(8, P).T
    perm = [4 * qd + 2 * i + e2 for qd in range(2) for e2 in range(2) for i in range(2)]
    sk = np.asarray(inp["attn_sinks"])[:, perm]
    sinks = np.ascontiguousarray(np.broadcast_to(sk.reshape(1, 16), (P, 16))).astype(np.float32)
    gw = np.zeros((P, 8, P), np.float32)
    for l in range(2):
        for gate, nm in enumerate(("lru_wa", "lru_wx")):
            for c in range(2):
                gi = (l * 2 + gate) * 2 + c
                for hh in range(2):
                    gw[64 * hh:64 * hh + 64, gi, 64 * hh:64 * hh + 64] = inp[nm][l][2 * c + hh]
    gw = gw.reshape(P, 8 * P)
    i = np.arange(P)[:, None]
    j = np.arange(P)[None, :]
    m_norm = np.concatenate([(j > i), (j <= i)], axis=1).astype(np.float32)
    m_first = np.concatenate([np.zeros((P, P), bool), (j <= i)], axis=1).astype(np.float32)
    masks = np.concatenate([m_norm, m_first], axis=1)
    ident = np.eye(P, dtype=np.float32)
    return pv, sinks, gw, masks, ident


_CACHE = {}


def _get_program(key, **kw):
    if key not in _CACHE:
        _CACHE[key] = build_program(**kw)
    return _CACHE[key]


def kernel(**inputs):
    inp = {k: np.asarray(v) for k, v in inputs.items()}
    x = inp["x"]
    B, S, _ = x.shape
    n_tiles = S // TT
    n_ctx = n_tiles // 2
    SH = n_ctx * TT
    nc = _get_program(("ctx", n_tiles, n_ctx), n_tiles=n_tiles, n_ctx=n_ctx, final_norm=True)
    wts = prep_weights(inp["w_in"], inp["w_out"], inp["w_up"], inp["w_down"], (0, 1))
    pv, sinks, gw, masks, ident = prep_small(inp)
    in_maps = []
    NCORES = 2 * B
    for c in range(NCORES):
        b, half = c // 2, c % 2
        pvc = pv.copy()
        if half == 1:
            xin = np.ascontiguousarray(x[b].T)
            pvc[:, O_FLAG] = 1.0
        else:
            xin = np.zeros((D, S), np.float32)
            xin[:, SH:] = x[b, :SH].T
            pvc[:, O_FLAG] = 0.0
        in_maps.append({"xT": xin, "wts": wts, "pvec": pvc, "sinks": sinks,
                        "gatew": gw, "masks": masks, "ident": ident})
    res = run_bass_kernel_spmd(nc, in_maps, core_ids=list(range(NCORES)))
    out = np.empty((B, S, D), np.float32)
    for c in range(NCORES):
        b, half = c // 2, c % 2
        out[b, half * SH:(half + 1) * SH] = res.results[c]["outT"].T
    return out
```

```python
import numpy as np
from contextlib import ExitStack
import concourse.bass as bass
import concourse.mybir as mybir
from concourse.bass_utils import run_bass_kernel_spmd

F32 = mybir.dt.float32
BF16 = mybir.dt.bfloat16
AF = mybir.ActivationFunctionType
ALU = mybir.AluOpType
AX = mybir.AxisListType
GELU = AF.Gelu_apprx_tanh

P = 128
D = 1024
KC = 8
TT = 512
NBLK = 4
UPL = 87
RF = 4
RG = 6
GRP = 6
PL = 108
NPV = 2 * PL + 9
O_FLAG = 2 * PL + 8
SAME_ENGINE_SYNC = True
LIST_SCHED = True
ENG_SCALE = {"pe": 0.95, "dve": 1.14, "act": 1.04, "pool": 1.25, "sp": 1.0}
XLAT = 1.2
FFN_PRIO_BIAS = 150

O_G1, O_G2, O_MG, O_CW, O_CB, O_LNG, O_LNB, O_LW, O_LB, O_BA, O_BX, O_LAM = (
    0, 8, 16, 24, 86, 88, 90, 92, 100, 102, 104, 106)


class Op:
    __slots__ = ("eng", "fn", "deps", "need_inc", "count", "dma_sem", "dur", "lat", "idx")

    def __init__(self, eng, fn, dma_sem, dur, lat):
        self.eng = eng
        self.fn = fn
        self.deps = set()
        self.need_inc = False
        self.count = 0
        self.dma_sem = dma_sem
        self.dur = dur
        self.lat = lat
        self.idx = 0


class Sched:
    def __init__(self):
        self.ops = []
        self.last_writer = {}
        self.readers = {}
        self.prio_bias = 0

    DUR = {"pe": 0.25, "act": 0.6, "dve": 0.68, "pool": 1.0, "sp": 0.15}

    def add(self, eng, fn, reads=(), writes=(), dma_sem=None, c=None, lat=None):
        if c is None:
            c = self.DUR[eng]
        c *= ENG_SCALE[eng]
        if lat is None:
            lat = 3.0 if dma_sem is not None else XLAT
        op = Op(eng, fn, dma_sem, c, lat)
        op.idx = len(self.ops) + self.prio_bias
        deps = op.deps
        for r in reads:
            lw = self.last_writer.get(r)
            if lw is not None:
                deps.add(lw)
        for w in writes:
            lw = self.last_writer.get(w)
            if lw is not None:
                deps.add(lw)
            for rd in self.readers.get(w, ()):
                deps.add(rd)
        for r in reads:
            self.readers.setdefault(r, []).append(op)
        for w in writes:
            self.last_writer[w] = op
            self.readers[w] = []
        deps.discard(op)
        self.ops.append(op)
        return op

    def list_schedule(self):
        ops = self.ops
        nd = {}
        users = {}
        for op in ops:
            nd[op] = len(op.deps)
            for d in op.deps:
                users.setdefault(d, []).append(op)
        rt = {}
        fin = {}
        cand = {}
        free = {}
        for op in ops:
            if nd[op] == 0:
                rt[op] = 0.0
                cand.setdefault(op.eng, []).append(op)
        order = {}
        n_left = len(ops)
        while n_left:
            best = None
            for e, lst in cand.items():
                if not lst:
                    continue
                te = free.get(e, 0.0)
                pick = None
                for op in lst:
                    r = rt[op]
                    key = (0.0, op.idx) if r <= te else (r - te, op.idx)
                    if pick is None or key < pick[0]:
                        pick = (key, op)
                op = pick[1]
                start = max(te, rt[op])
                if best is None or (start, op.idx) < best[0]:
                    best = ((start, op.idx), op, start)
            _, op, start = best
            e = op.eng
            cand[e].remove(op)
            f = start + op.dur
            free[e] = f
            fin[op] = f
            order.setdefault(e, []).append(op)
            n_left -= 1
            for u_ in users.get(op, ()):
                if u_.eng == e:
                    lat = 0.0 if e == "pe" else (op.lat if op.dma_sem is not None else 0.1)
                else:
                    lat = op.lat
                r = f + lat
                if r > rt.get(u_, 0.0):
                    rt[u_] = r
                nd[u_] -= 1
                if nd[u_] == 0:
                    cand.setdefault(u_.eng, []).append(u_)
        self.makespan = max(fin.values())
        return order

    def emit(self, nc, block, sems, dma_sems):
        ops = self.ops
        for op in ops:
            for d in op.deps:
                if d.eng == "pe" and op.eng == "pe":
                    continue
                if d.eng == op.eng and d.dma_sem is None and not SAME_ENGINE_SYNC:
                    continue
                d.need_inc = True
        if LIST_SCHED:
            by_eng = self.list_schedule()
        else:
            by_eng = {}
            for op in ops:
                by_eng.setdefault(op.eng, []).append(op)
        cnt = {}
        for op in ops:
            if op.dma_sem is not None:
                cnt[op.dma_sem] = cnt.get(op.dma_sem, 0) + 16
                op.count = cnt[op.dma_sem]
        for e, lst in by_eng.items():
            for op in lst:
                if op.dma_sem is None and op.need_inc:
                    cnt[e] = cnt.get(e, 0) + 1
                    op.count = cnt[e]

        def run(eng_name, eng):
            waited = {}
            for op in by_eng.get(eng_name, ()):
                need = {}
                for d in op.deps:
                    if d.dma_sem is not None:
                        key = ("d", d.dma_sem)
                    else:
                        if d.eng == "pe" and eng_name == "pe":
                            continue
                        if d.eng == eng_name and not SAME_ENGINE_SYNC:
                            continue
                        key = ("e", d.eng)
                    if d.count > need.get(key, 0):
                        need[key] = d.count
                for key, val in need.items():
                    if val > waited.get(key, 0):
                        sem = dma_sems[key[1]] if key[0] == "d" else sems[key[1]]
                        eng.wait_ge(sem, val)
                        waited[key] = val
                ins = op.fn(eng)
                if op.dma_sem is not None:
                    ins.then_inc(dma_sems[op.dma_sem], 16)
                elif op.need_inc:
                    ins.then_inc(sems[eng_name], 1)

        @block.tensor
        def _(e):
            run("pe", e)

        @block.scalar
        def _(e):
            run("act", e)

        @block.vector
        def _(e):
            run("dve", e)

        @block.gpsimd
        def _(e):
            run("pool", e)

        @block.sync
        def _(e):
            run("sp", e)


def build_program(n_tiles, n_ctx=0, final_norm=True):
    nc = bass.Bass("TRN2", target_bir_lowering=False)
    S = n_tiles * TT
    S_out = (n_tiles - n_ctx) * TT
    NL = 2
    xT = nc.dram_tensor("xT", [D, S], F32, kind="ExternalInput").ap()
    wts = nc.dram_tensor("wts", [NL * UPL * P, 1024], F32, kind="ExternalInput").ap()
    pvec_d = nc.dram_tensor("pvec", [P, NPV], F32, kind="ExternalInput").ap()
    sinks_d = nc.dram_tensor("sinks", [P, 16], F32, kind="ExternalInput").ap()
    gatew_d = nc.dram_tensor("gatew", [P, 8 * P], F32, kind="ExternalInput").ap()
    masks_d = nc.dram_tensor("masks", [P, 512], F32, kind="ExternalInput").ap()
    ident_d = nc.dram_tensor("ident", [P, P], F32, kind="ExternalInput").ap()
    outT = nc.dram_tensor("outT", [D, S_out], F32, kind="ExternalOutput").ap()
    wbf = nc.dram_tensor("wbf", [NL * UPL * P, 1024], BF16).ap()

    xT_v = xT.rearrange("(k p) s -> p k s", p=P)
    outT_v = outT.rearrange("(k p) s -> p k s", p=P)

    sch = Sched()
    es = ExitStack()

    def sb(name, shape, dt):
        return es.enter_context(nc.sbuf_tensor(name, shape, dt))

    def ps(name, shape, dt):
        return es.enter_context(nc.psum_tensor(name, shape, dt))

    with es:
        hbuf = [sb(f"h{i}", [P, KC, TT], F32) for i in range(2)]
        sq = sb("sq", [P, 2, TT], BF16)
        sqf = sb("sqf", [P, 2, TT], BF16)
        hn1 = sb("hn1", [P, KC, TT], BF16)
        hn2 = sb("hn2", [P, KC, TT], BF16)
        rt = sb("rt", [P, TT], F32)
        rstd = sb("rstd", [P, TT], F32)
        rtf = sb("rtf", [P, TT], F32)
        rstdf = sb("rstdf", [P, TT], F32)
        qT = sb("qT", [P, 4, TT], BF16)
        kk = [sb(f"kk{l}", [P, 2, P + TT], BF16) for l in range(2)]
        vtok = [sb(f"vtok{l}", [P, NBLK + 1, P], BF16) for l in range(2)]
        u = [sb(f"u{l}", [P, 2, 30 + TT], F32) for l in range(2)]
        rx = [sb(f"rx{l}", [P, 2, 3 + TT], F32) for l in range(2)]
        hst = [sb(f"hst{l}", [P, 2], F32) for l in range(2)]
        gg = sb("gg", [P, 2, TT], F32)
        cacc = sb("cacc", [P, 2, TT], F32)
        cbf = sb("cbf", [P, 2, TT], BF16)
        csq = sb("csq", [P, 2, TT], BF16)
        mu = sb("mu", [P, TT], F32)
        musq = sb("musq", [P, TT], F32)
        lrs = sb("lrs", [P, TT], F32)
        ysq = sb("ysq", [P, 2, TT], BF16)
        grs = sb("grs", [P, TT], F32)
        xc = sb("xc", [P, 2, TT], F32)
        xcb = sb("xcb", [P, 2, TT], BF16)
        ra = sb("ra", [P, 2, TT], F32)
        a2s = sb("a2s", [P, 2, TT], F32)
        igx = sb("igx", [P, 2, TT], F32)
        hs = sb("hs", [P, 2, TT], F32)
        t2 = a2s
        yc = igx
        sig = hs
        pm = sb("pm", [P, 8, 256], BF16)
        ptsb = [sb(f"ptsb{i}", [P, 8 * P], BF16) for i in range(2)]
        mraw = sb("mraw", [P, 8], F32)
        negm = sb("negm", [P, 2, 8], F32)
        rsum = sb("rsum", [P, 8], F32)
        tsk = sb("tsk", [P, 8], F32)
        esk = sb("esk", [P, 8], F32)
        den = sb("den", [P, 8], F32)
        rden = sb("rden", [P, 8], F32)
        yatt = sb("yatt", [P, 512], F32)
        yasq = sb("yasq", [P, 512], F32)
        ass = sb("ass", [P, 1], F32)
        asd = sb("asd", [P, 1], F32)
        ars = sb("ars", [P, 1], F32)
        yattn = sb("yattn", [P, 512], BF16)
        ymix = sb("ymix", [P, KC, TT], BF16)
        hidden = sb("hidden", [P, 16, TT], BF16)
        r32 = sb("r32", [P, 2, TT], F32)
        ob = sb("ob", [P, 2, TT], F32)
        ringF = sb("ringF", [P, RF, 1024], BF16)
        ringG = sb("ringG", [P, RG, 1024], BF16)
        pvec = sb("pvecs", [P, NPV], F32)
        sinks = sb("sinkss", [P, 16], F32)
        negsinks = sb("negsinks", [P, 16], F32)
        cneg = sb("cneg", [P, 4], F32)
        ctmp = sb("ctmp", [P, 4], F32)
        stage = sb("stage", [P, 8 * P], F32)
        gatew = sb("gatewb", [P, 8 * P], BF16)
        masks = sb("masksb", [P, 768], BF16)
        ident = sb("identb", [P, P], BF16)
        ones = sb("ones", [P, P], BF16)
        psA = [ps(f"psA{i}", [P, 512], F32) for i in range(4)]
        psS2 = ps("psS2", [P, 1024], F32)
        psS = [psS2[:, 0:512], psS2[:, 512:1024]]
        psT = ps("psT", [P, 1024], BF16)
        psO = ps("psO", [P, 512], F32)

        eng_names = ["pe", "act", "dve", "pool"]
        sems = {n: es.enter_context(nc.semaphore(f"sem_{n}")) for n in eng_names}
        n_units = NL * UPL
        gbounds = [0, 1, 3, 6]
        while gbounds[-1] < n_units:
            gbounds.append(min(n_units, gbounds[-1] + GRP))
        NGRP = len(gbounds) - 1
        grp_of = {}
        for g_ in range(NGRP):
            for u_ in range(gbounds[g_], gbounds[g_ + 1]):
                grp_of[u_] = g_
        dma_names = ([f"rF{i}" for i in range(RF)] + [f"rG{i}" for i in range(RG)] + ["xin0", "xin1", "out0", "out1"]
                     + [f"const{i}" for i in range(5)] + [f"pro{g}" for g in range(NGRP)])
        dma_sems = {n: es.enter_context(nc.semaphore(f"dsem_{n}")) for n in dma_names}

        add = sch.add
        ctrF = [0]
        ctrG = [0]

        def bankF():
            i = ctrF[0] % 2
            ctrF[0] += 1
            return i

        def bankG():
            i = 2 + ctrG[0] % 2
            ctrG[0] += 1
            return i

        add("sp", lambda e: e.dma_start(out=pvec[:], in_=pvec_d), writes=["pvec"], dma_sem="const0")
        add("sp", lambda e: e.dma_start(out=sinks[:], in_=sinks_d), writes=["sinks"], dma_sem="const1")
        add("sp", lambda e: e.dma_start(out=stage[:], in_=gatew_d), writes=["stage"], dma_sem="const2")
        add("dve", lambda e: e.tensor_copy(out=gatew[:], in_=stage[:]), reads=["stage"], writes=["gatew"])
        add("sp", lambda e: e.dma_start(out=stage[:, 0:512], in_=masks_d), writes=["stage"], dma_sem="const3")
        add("dve", lambda e: e.tensor_copy(out=masks[:, 0:512], in_=stage[:, 0:512]), reads=["stage"], writes=["masks"])
        add("dve", lambda e: e.tensor_scalar(out=masks[:, 512:640], in0=stage[:, 0:128], scalar1=pvec[:, O_FLAG:O_FLAG + 1],
                                            scalar2=None, op0=ALU.mult), reads=["stage", "pvec"], writes=["masks"])
        add("dve", lambda e: e.tensor_copy(out=masks[:, 640:768], in_=stage[:, 128:256]), reads=["stage"], writes=["masks"])
        add("sp", lambda e: e.dma_start(out=stage[:, 0:P], in_=ident_d), writes=["stage"], dma_sem="const4")
        add("dve", lambda e: e.tensor_copy(out=ident[:], in_=stage[:, 0:P]), reads=["stage"], writes=["ident"])
        add("dve", lambda e: e.memset(ones[:], 1.0), writes=["ones"])
        add("dve", lambda e: e.tensor_scalar(out=negsinks[:], in0=sinks[:], scalar1=-1.0, scalar2=None,
                                            op0=ALU.mult), reads=["sinks"], writes=["negsinks"])
        for l in range(2):
            lam = pvec[:, l * PL + O_LAM: l * PL + O_LAM + 2]
            add("act", lambda e, lam=lam, l=l: e.activation(out=ctmp[:, 2 * l:2 * l + 2], in_=lam, func=AF.Exp,
                                                          scale=-1.0),
                reads=["pvec"], writes=[f"ctmp{l}"])
            add("act", lambda e, l=l: e.activation(out=ctmp[:, 2 * l:2 * l + 2], in_=ctmp[:, 2 * l:2 * l + 2],
                                                 func=AF.Ln, bias=1.0),
                reads=[f"ctmp{l}"], writes=[f"ctmp{l}"])
            add("dve", lambda e, l=l: e.tensor_scalar(out=cneg[:, 2 * l:2 * l + 2], in0=ctmp[:, 2 * l:2 * l + 2],
                                                    scalar1=-8.0, scalar2=None, op0=ALU.mult),
                reads=[f"ctmp{l}"], writes=[f"cneg{l}"])
            add("pool", lambda e, l=l: e.memset(kk[l][:], 0.0), writes=[f"kk{l}_0", f"kk{l}_1"])
            add("pool", lambda e, l=l: e.memset(vtok[l][:], 0.0), writes=[f"vtok{l}"])
            add("pool", lambda e, l=l: e.memset(u[l][:], 0.0), writes=[f"u{l}_0", f"u{l}_1"])
            add("pool", lambda e, l=l: e.memset(rx[l][:], 0.0), writes=[f"rx{l}_0", f"rx{l}_1"])
            add("pool", lambda e, l=l: e.memset(hst[l][:], 0.0), writes=[f"hst{l}_0", f"hst{l}_1"])

        for g in range(NGRP):
            g0 = gbounds[g]
            g1 = gbounds[g + 1]
            add("pool", lambda e, g0=g0, g1=g1: e.dma_start(out=wbf[g0 * P:g1 * P, :], in_=wts[g0 * P:g1 * P, :]),
                writes=[f"wbf{g}", f"prowin{g % 3}"], dma_sem=f"pro{g}", c=1.0, lat=12.0)

        fronts_seq = []
        if n_ctx > 0:
            fronts_seq.append((0, 0, False))
            for t in range(1, n_ctx + 1):
                fronts_seq.append((t, 0, False))
                fronts_seq.append((t - 1, 1, True))
            t = n_ctx
            if t + 1 < n_tiles:
                fronts_seq += [(t + 1, 0, False), (t, 1, False), (t + 1, 1, False)]
                t += 2
            else:
                fronts_seq += [(t, 1, False)]
                t += 1
        else:
            t = 0
        while t < n_tiles:
            pair = [t] if t + 1 >= n_tiles else [t, t + 1]
            for l in range(2):
                for tt_ in pair:
                    fronts_seq.append((tt_, l, False))
            t += 2

        def punits(tt_):
            return range(4, 13) if tt_ == n_ctx - 1 else range(11, 13)

        def f_units():
            for (tt_, l, partial) in fronts_seq:
                for ui in (punits(tt_) if partial else range(23)):
                    yield l * UPL + ui

        def g_units():
            for (tt_, l, partial) in fronts_seq:
                if partial:
                    continue
                for half in range(2):
                    for j in range(16):
                        yield l * UPL + 23 + 16 * half + j
                    for i in range(KC):
                        for q in range(2):
                            yield l * UPL + 55 + i * 4 + 2 * half + q

        class Ring:
            def __init__(self, buf, nslots, tag, seq):
                self.buf, self.n, self.tag, self.seq = buf, nslots, tag, seq
                self.ctr = 0
                self.pending = []

            def _load(self, gu):
                slot = self.ctr % self.n
                self.ctr += 1
                buf, tag = self.buf, self.tag
                add("sp", lambda e, slot=slot, gu=gu, buf=buf: e.dma_start(out=buf[:, slot, :],
                                                                           in_=wbf[gu * P:(gu + 1) * P, :]),
                    reads=[f"wbf{grp_of[gu]}"], writes=[f"{tag}{slot}"], dma_sem=f"{tag}{slot}")
                return slot

            def prefetch(self):
                while len(self.pending) < self.n - 1:
                    try:
                        gu = next(self.seq)
                    except StopIteration:
                        break
                    self.pending.append(self._load(gu))

            def get(self):
                self.prefetch()
                slot = self.pending.pop(0)
                return slot

        RFr = Ring(ringF, RF, "rF", f_units())
        RGr = Ring(ringG, RG, "rG", g_units())

        def norm_stats(hb, sqb, sqtok, rtb, rttok, rsb, rstok, bank):
            h = hbuf[hb]
            for k in range(KC):
                s = k % 2
                add("act", lambda e, k=k, s=s: e.activation(out=sqb[:, s, :], in_=h[:, k, :], func=AF.Square),
                    reads=[f"h{hb}_{k}"], writes=[f"{sqtok}{s}"])
                add("pe", lambda e, k=k, s=s: e.matmul(psA[bank][:], lhsT=ones[:], rhs=sqb[:, s, :],
                                                     start=(k == 0), stop=(k == KC - 1)),
                    reads=[f"{sqtok}{s}", "ones"], writes=[f"psA{bank}"])
            add("act", lambda e: e.activation(out=rtb[:], in_=psA[bank][:], func=AF.Ln, scale=1.0 / D, bias=1e-6),
                reads=[f"psA{bank}"], writes=[rttok])
            add("act", lambda e: e.activation(out=rsb[:], in_=rtb[:], func=AF.Exp, scale=-0.5), reads=[rttok], writes=[rstok])

        def merge_gen(gens, totals, scale):
            acc = [0.0] * len(gens)
            alive = [True] * len(gens)
            while any(alive):
                i = min((acc[j] / totals[j], j) for j in range(len(gens)) if alive[j])[1]
                try:
                    c = next(gens[i])
                    acc[i] += c
                    yield c * scale
                except StopIteration:
                    alive[i] = False

        def front(t, l, partial, mk0):
            hb = t % 2
            h = hbuf[hb]
            pb = l * PL
            if l == 0:
                add("sp", lambda e: e.dma_start(out=h[:], in_=xT_v[:, :, t * TT:(t + 1) * TT]),
                    writes=[f"h{hb}_{k}" for k in range(KC)], dma_sem=f"xin{hb}", lat=12.0)
            norm_stats(hb, sq, "sq", rt, "rt", rstd, "rstd", bankF())
            yield 6.0
            for k in range(KC):
                add("dve", lambda e, k=k: e.scalar_tensor_tensor(out=hn1[:, k, :], in0=h[:, k, :],
                                                               scalar=pvec[:, pb + O_G1 + k:pb + O_G1 + k + 1],
                                                               in1=rstd[:], op0=ALU.mult, op1=ALU.mult),
                    reads=[f"h{hb}_{k}", "rstd", "pvec"], writes=[f"hn1_{k}"])
            yield 5.0
            hn_all = [f"hn1_{k}" for k in range(KC)]
            for j in (punits(t) if partial else range(15)):
                slot = RFr.get()
                bank = bankF()
                if j == 6:
                    for blk in range(NBLK):
                        for k in range(KC):
                            add("pe", lambda e, blk=blk, k=k, slot=slot, bank=bank: e.matmul(
                                psA[bank][:, blk * P:(blk + 1) * P], lhsT=hn1[:, k, blk * P:(blk + 1) * P],
                                rhs=ringF[:, slot, k * P:(k + 1) * P], start=(k == 0), stop=(k == KC - 1)),
                                reads=[f"rF{slot}"] + hn_all, writes=[f"psA{bank}"], c=(0.15 if j == 6 else 0.25))
                    add("act", lambda e, bank=bank: e.activation(
                        out=vtok[l][:, 1:NBLK + 1, :],
                        in_=psA[bank][:].rearrange("p (b c) -> p b c", b=NBLK), func=AF.Copy),
                        reads=[f"psA{bank}"], writes=[f"vtok{l}"])
                    yield 0.0
                    continue
                for k in range(KC):
                    add("pe", lambda e, k=k, slot=slot, bank=bank: e.matmul(
                        psA[bank][:], lhsT=ringF[:, slot, k * P:(k + 1) * P], rhs=hn1[:, k, :],
                        start=(k == 0), stop=(k == KC - 1)),
                        reads=[f"rF{slot}"] + hn_all, writes=[f"psA{bank}"], c=(0.15 if j == 6 else 0.25))
                if j < 4:
                    add("act", lambda e, j=j, bank=bank: e.activation(out=qT[:, j, :], in_=psA[bank][:],
                                                                    func=AF.Copy, scale=0.125),
                        reads=[f"psA{bank}"], writes=[f"qT{j}"])
                elif j < 6:
                    c = j - 4
                    add("act", lambda e, c=c, bank=bank: e.activation(out=kk[l][:, c, P:P + TT], in_=psA[bank][:],
                                                                    func=AF.Copy),
                        reads=[f"psA{bank}"], writes=[f"kk{l}_{c}"])
                elif j < 9:
                    c = j - 7
                    add("act", lambda e, c=c, bank=bank: e.activation(out=sig[:, c, :], in_=psA[bank][:],
                                                                    func=AF.Sigmoid),
                        reads=[f"psA{bank}"], writes=[f"hs{c}"])
                elif j < 11:
                    c = j - 9
                    add("dve", lambda e, c=c, bank=bank: e.tensor_tensor(out=u[l][:, c, 30:30 + TT], in0=psA[bank][:],
                                                                       in1=sig[:, c, :], op=ALU.mult),
                        reads=[f"psA{bank}", f"hs{c}"], writes=[f"u{l}_{c}"])
                elif j < 13:
                    c = j - 11
                    add("act", lambda e, c=c, bank=bank: e.activation(out=rx[l][:, c, 3:3 + TT], in_=psA[bank][:],
                                                                    func=AF.Copy),
                        reads=[f"psA{bank}"], writes=[f"rx{l}_{c}"])
                else:
                    c = j - 13
                    add("act", lambda e, c=c, bank=bank: e.activation(out=gg[:, c, :], in_=psA[bank][:],
                                                                    func=GELU),
                        reads=[f"psA{bank}"], writes=[f"gg{c}"])
                yield 0.0

            def attention():
                def stA(b, qd):
                    mk = mk0 if b == 0 else 0
                    bp = b % 2
                    for e2 in range(2):
                        for i in range(2):
                            c = 2 * qd + i
                            add("pe", lambda e, e2=e2, i=i, c=c: e.matmul(
                                psS[e2][:, i * 256:(i + 1) * 256], lhsT=qT[64 * e2:64 * e2 + 64, c, b * P:(b + 1) * P],
                                rhs=kk[l][64 * e2:64 * e2 + 64, qd, b * P:b * P + 256], start=True, stop=True),
                                reads=[f"qT{c}", f"kk{l}_{qd}"], writes=[f"psS{e2}"], c=0.17)
                    add("dve", lambda e: e.reduce_max(
                        out=mraw[:, 4 * qd:4 * qd + 4],
                        in_=psS2[:, :].rearrange("p (x k) -> p x k", x=4), axis=AX.X),
                        reads=["psS0", "psS1"], writes=[f"mrawq{qd}"], c=1.1)
                    add("dve", lambda e: e.scalar_tensor_tensor(
                        out=negm[:, bp, 4 * qd:4 * qd + 4], in0=mraw[:, 4 * qd:4 * qd + 4], scalar=-1.0,
                        in1=negsinks[:, 8 * l + 4 * qd:8 * l + 4 * qd + 4], op0=ALU.mult, op1=ALU.min),
                        reads=[f"mrawq{qd}", "negsinks"], writes=[f"negm{bp}q{qd}"], c=0.15)
                    for x in range(4):
                        pos = 4 * qd + x
                        e2 = x // 2
                        add("act", lambda e, x=x, pos=pos, e2=e2: e.activation(
                            out=pm[:, pos, :], in_=psS2[:, x * 256:(x + 1) * 256], func=AF.Exp,
                            bias=negm[:, bp, pos:pos + 1]),
                            reads=[f"psS{e2}", f"negm{bp}q{qd}"], writes=[f"pmq{qd}"], c=0.45)
                    add("pool", lambda e: e.tensor_tensor(
                        out=pm[:, 4 * qd:4 * qd + 4, :], in0=pm[:, 4 * qd:4 * qd + 4, :],
                        in1=masks[:, mk * 256:(mk + 1) * 256].unsqueeze(1).to_broadcast([P, 4, 256]), op=ALU.mult),
                        reads=[f"pmq{qd}", "masks"], writes=[f"pmq{qd}"], c=1.8)

                def stB(b, qd):
                    th = qd % 2
                    for x in range(4):
                        pos = 4 * qd + x
                        for half in range(2):
                            col = (2 * x + half) * P
                            add("pe", lambda e, pos=pos, half=half, col=col: e.transpose(
                                out=psT[:, col:col + P], in_=pm[:, pos, half * P:(half + 1) * P], identity=ident[:]),
                                reads=[f"pmq{qd}", "ident"], writes=["psT"], c=0.15)
                    add("act", lambda e: e.activation(out=ptsb[th][:], in_=psT[:, :], func=AF.Copy),
                        reads=["psT"], writes=[f"ptsb{th}"], c=1.1)
                    for x in range(4):
                        e2, i = x // 2, x % 2
                        hh = 4 * qd + 2 * i + e2
                        for half in range(2):
                            add("pe", lambda e, x=x, hh=hh, half=half: e.matmul(
                                psO[:, hh * 64:(hh + 1) * 64],
                                lhsT=ptsb[th][:, (2 * x + half) * P:(2 * x + half + 1) * P],
                                rhs=vtok[l][:, b + half, qd * 64:(qd + 1) * 64], start=(half == 0), stop=(half == 1)),
                                reads=[f"ptsb{th}", f"vtok{l}"], writes=["psO"], c=0.1)
                    add("dve", lambda e: e.reduce_sum(out=rsum[:, 4 * qd:4 * qd + 4], in_=pm[:, 4 * qd:4 * qd + 4, :],
                                                     axis=AX.X),
                        reads=[f"pmq{qd}"], writes=[f"rsumq{qd}"], c=1.1)

                def stT1(b):
                    bp = b % 2
                    add("dve", lambda e: e.tensor_tensor(out=tsk[:], in0=sinks[:, 8 * l:8 * l + 8], in1=negm[:, bp, :],
                                                       op=ALU.add),
                        reads=["sinks", f"negm{bp}q0", f"negm{bp}q1"], writes=["tsk"], c=0.15)
                    add("act", lambda e: e.activation(out=esk[:], in_=tsk[:], func=AF.Exp), reads=["tsk"],
                        writes=["esk"], c=0.2)
                    add("dve", lambda e: e.tensor_tensor(out=den[:], in0=rsum[:], in1=esk[:], op=ALU.add),
                        reads=["rsumq0", "rsumq1", "esk"], writes=["den"], c=0.15)
                    add("dve", lambda e: e.reciprocal(
                        out=rden[:].rearrange("p (q i e) -> p q e i", q=2, i=2, e=2),
                        in_=den[:].rearrange("p (q e i) -> p q e i", q=2, e=2, i=2)),
                        reads=["den"], writes=["rden"], c=0.15)
                    add("dve", lambda e: e.tensor_tensor(
                        out=yatt[:].rearrange("p (h d) -> p h d", h=8), in0=psO[:].rearrange("p (h d) -> p h d", h=8),
                        in1=rden[:].unsqueeze(2).to_broadcast([P, 8, 64]), op=ALU.mult),
                        reads=["psO", "rden"], writes=["yatt"])

                def stT2(b):
                    add("pool", lambda e: e.tensor_tensor(out=yasq[:], in0=yatt[:], in1=yatt[:], op=ALU.mult),
                        reads=["yatt"], writes=["yasq"])
                    add("dve", lambda e: e.reduce_sum(out=ass[:], in_=yasq[:], axis=AX.X), reads=["yasq"],
                        writes=["ass"])
                    add("act", lambda e: e.activation(out=asd[:], in_=ass[:], func=AF.Ln, scale=1.0 / 512,
                                                      bias=1e-6),
                        reads=["ass"], writes=["asd"], c=0.15)
                    add("act", lambda e: e.activation(out=ars[:], in_=asd[:], func=AF.Exp, scale=-0.5), reads=["asd"], writes=["ars"], c=0.15)
                    add("act", lambda e: e.activation(out=yattn[:], in_=yatt[:], func=AF.Copy, scale=ars[:, 0:1]),
                        reads=["yatt", "ars"], writes=["yattn"])

                def stT3(b):
                    for cc in range(4):
                        add("pe", lambda e, cc=cc: e.transpose(out=psT[:, cc * P:(cc + 1) * P],
                                                             in_=yattn[:, cc * P:(cc + 1) * P], identity=ident[:]),
                            reads=["yattn", "ident"], writes=["psT"], c=0.15)
                    for cc in range(4):
                        add("act", lambda e, cc=cc: e.activation(
                            out=ymix[:, cc, b * P:(b + 1) * P], in_=psT[:, cc * P:(cc + 1) * P], func=AF.Copy,
                            scale=pvec[:, pb + O_MG + cc:pb + O_MG + cc + 1]),
                            reads=["psT", "pvec"], writes=[f"ymix{cc}"], c=0.3)

                NQ = 2 * NBLK
                tails = []
                stA(0, 0)
                yield 3.0
                for p in range(NQ):
                    if p + 1 < NQ:
                        stA((p + 1) // 2, (p + 1) % 2)
                        yield 3.0
                    if tails:
                        fn, cst = tails.pop(0)
                        fn()
                        yield cst
                    stB(p // 2, p % 2)
                    yield 3.0
                    if p % 2 == 1:
                        bb = p // 2
                        tails += [(lambda bb=bb: stT1(bb), 2.0), (lambda bb=bb: stT2(bb), 2.5),
                                  (lambda bb=bb: stT3(bb), 1.0)]
                    if tails:
                        fn, cst = tails.pop(0)
                        fn()
                        yield cst
                while tails:
                    fn, cst = tails.pop(0)
                    fn()
                    yield cst

            def lru_branch():
                for c in range(2):
                    wb = pb + O_LW + 4 * c
                    add("dve", lambda e, c=c, wb=wb: e.tensor_scalar(
                        out=xc[:, c, :], in0=rx[l][:, c, 0:TT], scalar1=pvec[:, wb:wb + 1],
                        scalar2=pvec[:, pb + O_LB + c:pb + O_LB + c + 1], op0=ALU.mult, op1=ALU.add),
                        reads=[f"rx{l}_{c}", "pvec"], writes=[f"xc{c}"])
                    for kq in range(1, 4):
                        add("dve", lambda e, c=c, wb=wb, kq=kq: e.scalar_tensor_tensor(
                            out=xc[:, c, :], in0=rx[l][:, c, kq:kq + TT], scalar=pvec[:, wb + kq:wb + kq + 1],
                            in1=xc[:, c, :], op0=ALU.mult, op1=ALU.add),
                            reads=[f"rx{l}_{c}", "pvec", f"xc{c}"], writes=[f"xc{c}"])
                    add("pool", lambda e, c=c: e.tensor_copy(out=rx[l][:, c, 0:3], in_=rx[l][:, c, TT:TT + 3]),
                        reads=[f"rx{l}_{c}"], writes=[f"rx{l}_{c}"])
                    add("pool", lambda e, c=c: e.tensor_copy(out=xcb[:, c, :], in_=xc[:, c, :]),
                        reads=[f"xc{c}"], writes=[f"xcb{c}"])
                    yield 3.0
                    for gate in range(2):
                        bank = bankF()
                        gi = (l * 2 + gate) * 2 + c
                        add("pe", lambda e, c=c, gi=gi, bank=bank: e.matmul(
                            psA[bank][:], lhsT=gatew[:, gi * P:(gi + 1) * P], rhs=xcb[:, c, :], start=True, stop=True),
                            reads=[f"xcb{c}", "gatew"], writes=[f"psA{bank}"])
                        dst = ra if gate == 0 else igx
                        dtok = f"ra{c}" if gate == 0 else f"igx{c}"
                        bo = pb + (O_BA if gate == 0 else O_BX) + c
                        add("act", lambda e, c=c, dst=dst, bo=bo, bank=bank: e.activation(
                            out=dst[:, c, :], in_=psA[bank][:], func=AF.Sigmoid, bias=pvec[:, bo:bo + 1]),
                            reads=[f"psA{bank}", "pvec"], writes=[dtok])
                    add("act", lambda e, c=c: e.activation(out=ra[:, c, :], in_=ra[:, c, :], func=AF.Exp,
                                                         scale=cneg[:, 2 * l + c:2 * l + c + 1]),
                        reads=[f"ra{c}", f"cneg{l}"], writes=[f"ra{c}"])
                    add("pool", lambda e, c=c: e.tensor_tensor(out=a2s[:, c, :], in0=ra[:, c, :], in1=ra[:, c, :],
                                                             op=ALU.mult),
                        reads=[f"ra{c}"], writes=[f"a2s{c}"])
                    add("act", lambda e, c=c: e.activation(out=a2s[:, c, :], in_=a2s[:, c, :], func=AF.Ln,
                                                         scale=-1.0, bias=1.0),
                        reads=[f"a2s{c}"], writes=[f"a2s{c}"])
                    add("act", lambda e, c=c: e.activation(out=a2s[:, c, :], in_=a2s[:, c, :], func=AF.Exp, scale=0.5),
                        reads=[f"a2s{c}"], writes=[f"a2s{c}"])
                    yield 2.0
                    add("pool", lambda e, c=c: e.tensor_tensor(out=igx[:, c, :], in0=igx[:, c, :], in1=xc[:, c, :],
                                                             op=ALU.mult),
                        reads=[f"igx{c}", f"xc{c}"], writes=[f"igx{c}"])
                    add("pool", lambda e, c=c: e.tensor_tensor(out=igx[:, c, :], in0=igx[:, c, :], in1=a2s[:, c, :],
                                                             op=ALU.mult),
                        reads=[f"igx{c}", f"a2s{c}"], writes=[f"igx{c}"])
                    add("dve", lambda e, c=c: e.tensor_tensor_scan(out=hs[:, c, :], data0=ra[:, c, :],
                                                                 data1=igx[:, c, :], initial=hst[l][:, c:c + 1],
                                                                 op0=ALU.mult, op1=ALU.add),
                        reads=[f"ra{c}", f"igx{c}", f"hst{l}_{c}"], writes=[f"hs{c}"], c=1.3)
                    add("pool", lambda e, c=c: e.tensor_copy(out=hst[l][:, c:c + 1], in_=hs[:, c, TT - 1:TT]),
                        reads=[f"hs{c}"], writes=[f"hst{l}_{c}"], c=0.2)
                    if partial:
                        yield 2.0
                        continue
                    add("pool", lambda e, c=c: e.tensor_tensor(out=gg[:, c, :], in0=hs[:, c, :], in1=gg[:, c, :],
                                                             op=ALU.mult),
                        reads=[f"hs{c}", f"gg{c}"], writes=[f"gg{c}"])
                    add("act", lambda e, c=c: e.activation(out=ysq[:, c, :], in_=gg[:, c, :], func=AF.Square),
                        reads=[f"gg{c}"], writes=[f"ysq{c}"])
                    yield 3.0
                if partial:
                    return
                bank = bankF()
                for c in range(2):
                    add("pe", lambda e, c=c, bank=bank: e.matmul(psA[bank][:], lhsT=ones[:], rhs=ysq[:, c, :],
                                                               start=(c == 0), stop=(c == 1)),
                        reads=[f"ysq{c}", "ones"], writes=[f"psA{bank}"])
                add("act", lambda e, bank=bank: e.activation(out=grs[:], in_=psA[bank][:], func=AF.Ln,
                                                           scale=1.0 / 256, bias=1e-6),
                    reads=[f"psA{bank}"], writes=["grs"])
                add("act", lambda e: e.activation(out=grs[:], in_=grs[:], func=AF.Exp, scale=-0.5), reads=["grs"], writes=["grs"])
                for c in range(2):
                    add("dve", lambda e, c=c: e.scalar_tensor_tensor(
                        out=ymix[:, 6 + c, :], in0=gg[:, c, :], scalar=pvec[:, pb + O_MG + 6 + c:pb + O_MG + 7 + c],
                        in1=grs[:], op0=ALU.mult, op1=ALU.mult),
                        reads=[f"gg{c}", "grs", "pvec"], writes=[f"ymix{6 + c}"])
                yield 2.5

            def conv_taps():
                for kq in range(31):
                    for c in range(2):
                        wcol = pb + O_CW + c * 31 + kq
                        if kq == 0:
                            add("dve", lambda e, c=c, wcol=wcol: e.tensor_scalar(
                                out=cacc[:, c, :], in0=u[l][:, c, 0:TT], scalar1=pvec[:, wcol:wcol + 1],
                                scalar2=pvec[:, pb + O_CB + c:pb + O_CB + c + 1], op0=ALU.mult, op1=ALU.add),
                                reads=[f"u{l}_{c}", "pvec"], writes=[f"cacc{c}"])
                        else:
                            add("dve", lambda e, c=c, wcol=wcol, kq=kq: e.scalar_tensor_tensor(
                                out=cacc[:, c, :], in0=u[l][:, c, kq:kq + TT], scalar=pvec[:, wcol:wcol + 1],
                                in1=cacc[:, c, :], op0=ALU.mult, op1=ALU.add),
                                reads=[f"u{l}_{c}", "pvec", f"cacc{c}"], writes=[f"cacc{c}"])
                    yield 1.25

            def conv_rest():
                for c in range(2):
                    add("pool", lambda e, c=c: e.tensor_copy(out=u[l][:, c, 0:30], in_=u[l][:, c, TT:TT + 30]),
                        reads=[f"u{l}_{c}"], writes=[f"u{l}_{c}"])
                    add("pool", lambda e, c=c: e.tensor_copy(out=cbf[:, c, :], in_=cacc[:, c, :]),
                        reads=[f"cacc{c}"], writes=[f"cbf{c}"])
                    add("act", lambda e, c=c: e.activation(out=csq[:, c, :], in_=cacc[:, c, :], func=AF.Square),
                        reads=[f"cacc{c}"], writes=[f"csq{c}"])
                yield 1.5
                b1 = bankF()
                b2 = bankF()
                for c in range(2):
                    add("pe", lambda e, c=c: e.matmul(psA[b1][:], lhsT=ones[:], rhs=cbf[:, c, :],
                                                    start=(c == 0), stop=(c == 1)),
                        reads=[f"cbf{c}", "ones"], writes=[f"psA{b1}"])
                for c in range(2):
                    add("pe", lambda e, c=c: e.matmul(psA[b2][:], lhsT=ones[:], rhs=csq[:, c, :],
                                                    start=(c == 0), stop=(c == 1)),
                        reads=[f"csq{c}", "ones"], writes=[f"psA{b2}"])
                add("act", lambda e: e.activation(out=mu[:], in_=psA[b1][:], func=AF.Copy, scale=1.0 / 256),
                    reads=[f"psA{b1}"], writes=["mu"])
                add("dve", lambda e: e.tensor_tensor(out=musq[:], in0=mu[:], in1=mu[:], op=ALU.mult),
                    reads=["mu"], writes=["musq"])
                add("dve", lambda e: e.scalar_tensor_tensor(out=lrs[:], in0=psA[b2][:], scalar=1.0 / 256,
                                                          in1=musq[:], op0=ALU.mult, op1=ALU.subtract),
                    reads=[f"psA{b2}", "musq"], writes=["lrs"])
                add("act", lambda e: e.activation(out=lrs[:], in_=lrs[:], func=AF.Ln, bias=1e-5),
                    reads=["lrs"], writes=["lrs"])
                add("act", lambda e: e.activation(out=lrs[:], in_=lrs[:], func=AF.Exp, scale=-0.5), reads=["lrs"], writes=["lrs"])
                yield 3.0
                for c in range(2):
                    add("pool", lambda e, c=c: e.tensor_tensor(out=t2[:, c, :], in0=cacc[:, c, :], in1=mu[:],
                                                             op=ALU.subtract),
                        reads=[f"cacc{c}", "mu"], writes=[f"a2s{c}"])
                    add("pool", lambda e, c=c: e.tensor_tensor(out=t2[:, c, :], in0=t2[:, c, :], in1=lrs[:],
                                                             op=ALU.mult),
                        reads=[f"a2s{c}", "lrs"], writes=[f"a2s{c}"])
                    add("act", lambda e, c=c: e.activation(
                        out=yc[:, c, :], in_=t2[:, c, :], func=AF.Silu,
                        scale=pvec[:, pb + O_LNG + c:pb + O_LNG + c + 1],
                        bias=pvec[:, pb + O_LNB + c:pb + O_LNB + c + 1]),
                        reads=[f"a2s{c}", "pvec"], writes=[f"igx{c}"])
                    add("act", lambda e, c=c: e.activation(out=ysq[:, c, :], in_=yc[:, c, :], func=AF.Square),
                        reads=[f"igx{c}"], writes=[f"ysq{c}"])
                yield 3.0
                bank = bankF()
                for c in range(2):
                    add("pe", lambda e, c=c, bank=bank: e.matmul(psA[bank][:], lhsT=ones[:], rhs=ysq[:, c, :],
                                                               start=(c == 0), stop=(c == 1)),
                        reads=[f"ysq{c}", "ones"], writes=[f"psA{bank}"])
                add("act", lambda e, bank=bank: e.activation(out=grs[:], in_=psA[bank][:], func=AF.Ln,
                                                           scale=1.0 / 256, bias=1e-6),
                    reads=[f"psA{bank}"], writes=["grs"])
                add("act", lambda e: e.activation(out=grs[:], in_=grs[:], func=AF.Exp, scale=-0.5), reads=["grs"], writes=["grs"])
                for c in range(2):
                    add("dve", lambda e, c=c: e.scalar_tensor_tensor(
                        out=ymix[:, 4 + c, :], in0=yc[:, c, :], scalar=pvec[:, pb + O_MG + 4 + c:pb + O_MG + 5 + c],
                        in1=grs[:], op0=ALU.mult, op1=ALU.mult),
                        reads=[f"igx{c}", "grs", "pvec"], writes=[f"ymix{4 + c}"])
                yield 2.5

            if partial:
                yield from lru_branch()
                if t != n_ctx - 1:
                    return
                for c in range(2):
                    add("pool", lambda e, c=c: e.tensor_copy(out=u[l][:, c, 0:30], in_=u[l][:, c, TT:TT + 30]),
                        reads=[f"u{l}_{c}"], writes=[f"u{l}_{c}"])
            else:
                yield from merge_gen([lru_branch(), conv_taps(), attention()], [18.5, 39.0, 70.0], 0.75)
                yield from conv_rest()
            for g in range(2):
                add("pool", lambda e, g=g: e.tensor_copy(out=kk[l][:, g, 0:P], in_=kk[l][:, g, TT:TT + P]),
                    reads=[f"kk{l}_{g}"], writes=[f"kk{l}_{g}"])
            add("pool", lambda e: e.tensor_copy(out=vtok[l][:, 0, :], in_=vtok[l][:, NBLK, :]),
                reads=[f"vtok{l}"], writes=[f"vtok{l}"])

            if partial:
                return
            ymix_all = [f"ymix{k}" for k in range(KC)]
            for i in range(KC):
                slot = RFr.get()
                bank = bankF()
                for k in range(KC):
                    add("pe", lambda e, k=k, slot=slot, bank=bank: e.matmul(
                        psA[bank][:], lhsT=ringF[:, slot, k * P:(k + 1) * P], rhs=ymix[:, k, :],
                        start=(k == 0), stop=(k == KC - 1)),
                        reads=[f"rF{slot}"] + ymix_all, writes=[f"psA{bank}"])
                add("dve", lambda e, i=i, bank=bank: e.tensor_tensor(out=h[:, i, :], in0=psA[bank][:], in1=h[:, i, :],
                                                                   op=ALU.add),
                    reads=[f"psA{bank}", f"h{hb}_{i}"], writes=[f"h{hb}_{i}"])
                yield 0.0
            norm_stats(hb, sqf, "sqf", rtf, "rtf", rstdf, "rstdf", bankF())
            yield 6.0
            for k in range(KC):
                add("dve", lambda e, k=k: e.scalar_tensor_tensor(out=hn2[:, k, :], in0=h[:, k, :],
                                                               scalar=pvec[:, pb + O_G2 + k:pb + O_G2 + k + 1],
                                                               in1=rstdf[:], op0=ALU.mult, op1=ALU.mult),
                    reads=[f"h{hb}_{k}", "rstdf", "pvec"], writes=[f"hn2_{k}"])
            yield 5.0

        def ffn(t, l, do_final):
            hb = t % 2
            h = hbuf[hb]
            hn_all = [f"hn2_{k}" for k in range(KC)]
            for half in range(2):
                for jj in range(16):
                    slot = RGr.get()
                    bank = bankG()
                    for k in range(KC):
                        add("pe", lambda e, k=k, slot=slot, bank=bank: e.matmul(
                            psA[bank][:], lhsT=ringG[:, slot, k * P:(k + 1) * P], rhs=hn2[:, k, :],
                            start=(k == 0), stop=(k == KC - 1)),
                            reads=[f"rG{slot}"] + hn_all, writes=[f"psA{bank}"])
                    rs_ = jj % 2
                    add("act", lambda e, rs_=rs_, bank=bank: e.activation(out=r32[:, rs_, :], in_=psA[bank][:],
                                                                        func=AF.Relu),
                        reads=[f"psA{bank}"], writes=[f"r32{rs_}"])
                    add("act", lambda e, rs_=rs_, jj=jj: e.activation(out=hidden[:, jj, :], in_=r32[:, rs_, :],
                                                                    func=AF.Square),
                        reads=[f"r32{rs_}"], writes=[f"hid{jj}"])
                    yield 1.95
                for i in range(KC):
                    bank = bankG()
                    for q in range(2):
                        slot = RGr.get()
                        for k2 in range(8):
                            jj = 8 * q + k2
                            add("pe", lambda e, k2=k2, jj=jj, q=q, slot=slot, bank=bank: e.matmul(
                                psA[bank][:], lhsT=ringG[:, slot, k2 * P:(k2 + 1) * P], rhs=hidden[:, jj, :],
                                start=(q == 0 and k2 == 0), stop=(q == 1 and k2 == 7)),
                                reads=[f"rG{slot}", f"hid{jj}"], writes=[f"psA{bank}"])
                        yield 1.95
                    add("dve", lambda e, i=i, bank=bank: e.tensor_tensor(out=h[:, i, :], in0=psA[bank][:],
                                                                       in1=h[:, i, :], op=ALU.add),
                        reads=[f"psA{bank}", f"h{hb}_{i}"], writes=[f"h{hb}_{i}"])
            if do_final:
                norm_stats(hb, sqf, "sqf", rtf, "rtf", rstdf, "rstdf", bankG())
                yield 4.0
                gb = 2 * PL
                for k in range(KC):
                    o = k % 2
                    add("dve", lambda e, k=k, o=o: e.scalar_tensor_tensor(
                        out=ob[:, o, :], in0=h[:, k, :], scalar=pvec[:, gb + k:gb + k + 1], in1=rstdf[:],
                        op0=ALU.mult, op1=ALU.mult),
                        reads=[f"h{hb}_{k}", "rstdf", "pvec"], writes=[f"ob{o}"])
                    add("sp", lambda e, k=k, o=o: e.dma_start(out=outT_v[:, k, (t - n_ctx) * TT:(t - n_ctx + 1) * TT],
                                                            in_=ob[:, o, :]),
                        reads=[f"ob{o}"], writes=[f"outdram{o}"], dma_sem=f"out{o}")
                yield 4.0

        ffn_q = []
        ffn_done = set()

        def ffn_step():
            key, gen = ffn_q[0]
            sch.prio_bias = FFN_PRIO_BIAS
            try:
                return next(gen)
            except StopIteration:
                ffn_done.add(key)
                ffn_q.pop(0)
                return 0.0
            finally:
                sch.prio_bias = 0

        def drain_until(key):
            while key not in ffn_done:
                assert ffn_q, key
                ffn_step()

        def run_front(gB):
            tA = tB = 0.0
            while True:
                if ffn_q and tA < tB:
                    tA += ffn_step()
                else:
                    try:
                        tB += next(gB)
                    except StopIteration:
                        break

        def reset_states(l):
            fl = pvec[:, O_FLAG:O_FLAG + 1]
            for g in range(2):
                add("pool", lambda e, g=g: e.tensor_scalar(out=kk[l][:, g, 0:P], in0=kk[l][:, g, 0:P], scalar1=fl,
                                                         scalar2=None, op0=ALU.mult),
                    reads=[f"kk{l}_{g}", "pvec"], writes=[f"kk{l}_{g}"])
                add("pool", lambda e, g=g: e.tensor_scalar(out=u[l][:, g, 0:30], in0=u[l][:, g, 0:30], scalar1=fl,
                                                         scalar2=None, op0=ALU.mult),
                    reads=[f"u{l}_{g}", "pvec"], writes=[f"u{l}_{g}"])
                add("pool", lambda e, g=g: e.tensor_scalar(out=rx[l][:, g, 0:3], in0=rx[l][:, g, 0:3], scalar1=fl,
                                                         scalar2=None, op0=ALU.mult),
                    reads=[f"rx{l}_{g}", "pvec"], writes=[f"rx{l}_{g}"])
                add("pool", lambda e, g=g: e.tensor_scalar(out=hst[l][:, g:g + 1], in0=hst[l][:, g:g + 1], scalar1=fl,
                                                         scalar2=None, op0=ALU.mult),
                    reads=[f"hst{l}_{g}", "pvec"], writes=[f"hst{l}_{g}"])
            add("pool", lambda e: e.tensor_scalar(out=vtok[l][:, 0, :], in0=vtok[l][:, 0, :], scalar1=fl,
                                                scalar2=None, op0=ALU.mult),
                reads=[f"vtok{l}", "pvec"], writes=[f"vtok{l}"])

        for (t, l, partial) in fronts_seq:
            if l == 1:
                drain_until((t, 0))
            elif t >= 2 and (t - 2) >= n_ctx:
                drain_until((t - 2, 1))
            if n_ctx > 0 and t == n_ctx:
                reset_states(l)
            mk0 = 1 if t == 0 else (2 if (n_ctx > 0 and t == n_ctx) else 0)
            run_front(front(t, l, partial, mk0))
            if not partial:
                ffn_q.append(((t, l), ffn(t, l, do_final=(l == 1 and final_norm))))
        while ffn_q:
            ffn_step()
        add("sp", lambda e: e.nop(), reads=["outdram0", "outdram1"], writes=["done"])

        with nc.Block() as block:
            sch.emit(nc, block, sems, dma_sems)
    return nc


def _unit(mat_rows_by_cols):
    return np.ascontiguousarray(mat_rows_by_cols.reshape(8, P, P).transpose(1, 0, 2)).reshape(P, 1024)


def prep_weights(w_in, w_out, w_up, w_down, layers):
    out = np.empty((len(layers), UPL, P, 1024), np.float32)
    for li, l in enumerate(layers):
        wi = w_in[l]
        col_sets = []
        for j in range(4):
            col_sets.append(np.arange(j * 128, (j + 1) * 128))
        k0 = np.arange(512, 576)
        k1 = np.arange(576, 640)
        col_sets.append(np.concatenate([k0, k0]))
        col_sets.append(np.concatenate([k1, k1]))
        col_sets.append(np.arange(640, 768))
        col_sets += [np.arange(1024 + c * 128, 1024 + (c + 1) * 128) for c in range(2)]
        col_sets += [np.arange(768 + c * 128, 768 + (c + 1) * 128) for c in range(2)]
        col_sets += [np.arange(1280 + c * 128, 1280 + (c + 1) * 128) for c in range(2)]
        col_sets += [np.arange(1536 + c * 128, 1536 + (c + 1) * 128) for c in range(2)]
        ui = 0
        for cs in col_sets:
            out[li, ui] = _unit(wi[:, cs])
            ui += 1
        for i in range(8):
            out[li, ui] = _unit(w_out[l][:, i * 128:(i + 1) * 128])
            ui += 1
        for j in range(32):
            out[li, ui] = _unit(w_up[l][:, j * 128:(j + 1) * 128])
            ui += 1
        for i in range(8):
            for q in range(4):
                out[li, ui] = _unit(w_down[l][q * 1024:(q + 1) * 1024, i * 128:(i + 1) * 128])
                ui += 1
        assert ui == UPL
    return out.reshape(len(layers) * UPL * P, 1024)


def prep_small(inp):
    pv = np.zeros((P, NPV), np.float32)
    for l in range(2):
        b = l * PL
        pv[:, b + O_G1:b + O_G1 + 8] = inp["norm1"][l].reshape(8, P).T
        pv[:, b + O_G2:b + O_G2 + 8] = inp["norm2"][l].reshape(8, P).T
        pv[:, b + O_MG:b + O_MG + 8] = inp["mix_norm"][l].reshape(8, P).T
        for c in range(2):
            pv[:, b + O_CW + c * 31:b + O_CW + (c + 1) * 31] = inp["conv_dw_w"][l][:, c * P:(c + 1) * P].T
            pv[:, b + O_LW + c * 4:b + O_LW + (c + 1) * 4] = inp["lru_conv_w"][l][:, c * P:(c + 1) * P].T
        pv[:, b + O_CB:b + O_CB + 2] = inp["conv_dw_b"][l].reshape(2, P).T
        pv[:, b + O_LNG:b + O_LNG + 2] = inp["conv_ln_g"][l].reshape(2, P).T
        pv[:, b + O_LNB:b + O_LNB + 2] = inp["conv_ln_b"][l].reshape(2, P).T
        pv[:, b + O_LB:b + O_LB + 2] = inp["lru_conv_b"][l].reshape(2, P).T
        pv[:, b + O_BA:b + O_BA + 2] = inp["lru_ba"][l].reshape(2, P).T
        pv[:, b + O_BX:b + O_BX + 2] = inp["lru_bx"][l].reshape(2, P).T
        pv[:, b + O_LAM:b + O_LAM + 2] = inp["lru_lambda"][l].reshape(2, P).T
    pv[:, 2 * PL:2 * PL + 8] = inp["final_norm"].reshape(8, P).T
    perm = [4 * qd + 2 * i + e2 for qd in range(2) for e2 in range(2) for i in range(2)]
    sk = np.asarray(inp["attn_sinks"])[:, perm]
    sinks = np.ascontiguousarray(np.broadcast_to(sk.reshape(1, 16), (P, 16))).astype(np.float32)
    gw = np.zeros((P, 8, P), np.float32)
    for l in range(2):
        for gate, nm in enumerate(("lru_wa", "lru_wx")):
            for c in range(2):
                gi = (l * 2 + gate) * 2 + c
                for hh in range(2):
                    gw[64 * hh:64 * hh + 64, gi, 64 * hh:64 * hh + 64] = inp[nm][l][2 * c + hh]
    gw = gw.reshape(P, 8 * P)
    i = np.arange(P)[:, None]
    j = np.arange(P)[None, :]
    m_norm = np.concatenate([(j > i), (j <= i)], axis=1).astype(np.float32)
    m_first = np.concatenate([np.zeros((P, P), bool), (j <= i)], axis=1).astype(np.float32)
    masks = np.concatenate([m_norm, m_first], axis=1)
    ident = np.eye(P, dtype=np.float32)
    return pv, sinks, gw, masks, ident


_CACHE = {}


def _get_program(key, **kw):
    if key not in _CACHE:
        _CACHE[key] = build_program(**kw)
    return _CACHE[key]


def kernel(**inputs):
    inp = {k: np.asarray(v) for k, v in inputs.items()}
    x = inp["x"]
    B, S, _ = x.shape
    n_tiles = S // TT
    n_ctx = n_tiles // 2
    SH = n_ctx * TT
    nc = _get_program(("ctx", n_tiles, n_ctx), n_tiles=n_tiles, n_ctx=n_ctx, final_norm=True)
    wts = prep_weights(inp["w_in"], inp["w_out"], inp["w_up"], inp["w_down"], (0, 1))
    pv, sinks, gw, masks, ident = prep_small(inp)
    in_maps = []
    NCORES = 2 * B
    for c in range(NCORES):
        b, half = c // 2, c % 2
        pvc = pv.copy()
        if half == 1:
            xin = np.ascontiguousarray(x[b].T)
            pvc[:, O_FLAG] = 1.0
        else:
            xin = np.zeros((D, S), np.float32)
            xin[:, SH:] = x[b, :SH].T
            pvc[:, O_FLAG] = 0.0
        in_maps.append({"xT": xin, "wts": wts, "pvec": pvc, "sinks": sinks,
                        "gatew": gw, "masks": masks, "ident": ident})
    res = run_bass_kernel_spmd(nc, in_maps, core_ids=list(range(NCORES)))
    out = np.empty((B, S, D), np.float32)
    for c in range(NCORES):
        b, half = c // 2, c % 2
        out[b, half * SH:(half + 1) * SH] = res.results[c]["outT"].T
    return out
```
